# Optimizing a Trainium2 kernel written in Bass

```python
import math
import jax, jax.numpy as jnp
from jax import lax
import numpy as np

D_MODEL = 2048
BATCH = 1
SEQ = 8192
DEPTH = 1

CHUNK = 64
MEM_LEN = 256
GM_BLOCK = 128
GM_GROUPS = 8
GM_WIDTH = 1024
GM_GROUP_DIM = GM_WIDTH // GM_GROUPS
SB_HEADS = 8
SB_HEAD_DIM = 128
SB_WIDTH = SB_HEADS * SB_HEAD_DIM
SB_QBLOCK = 128
MIX_WIDTH = GM_WIDTH + SB_WIDTH
IN_WIDTH = 2 * GM_WIDTH + 3 * SB_WIDTH
MEM_HEADS = 4
MEM_HEAD_DIM = D_MODEL // MEM_HEADS
N_GROUPS = 8
EXPERTS_PER_GROUP = 8
N_EXPERTS = N_GROUPS * EXPERTS_PER_GROUP
TOP_K = 2
D_EXPERT = D_MODEL // 4
MOE_BLOCK = 128
DN_ALPHA = (2 * DEPTH) ** 0.25
DN_BETA = (8 * DEPTH) ** -0.25
LN_EPS = 1e-5

kernel_name = "hybrid_gmlp_stickbreak_memxattn_hmoe"


def layer_norm(x, g, b):
    xf = x.astype(jnp.float32)
    mu = jnp.mean(xf, axis=-1, keepdims=True)
    xc = xf - mu
    var = jnp.mean(xc * xc, axis=-1, keepdims=True)
    return (xc * lax.rsqrt(var + LN_EPS) * g.astype(jnp.float32) + b.astype(jnp.float32)).astype(x.dtype)


def chunked_spatial_gating(u, v, ln_g, ln_b, w_s, b_s):
    b, s, _ = u.shape
    n = s // GM_BLOCK
    vg = layer_norm(v.reshape(b, s, GM_GROUPS, GM_GROUP_DIM), ln_g, ln_b)
    vg = vg.reshape(b, n, GM_BLOCK, GM_GROUPS, GM_GROUP_DIM)
    chunk_id = jnp.arange(GM_BLOCK) // CHUNK
    mask = chunk_id[:, None] >= chunk_id[None, :]
    w = jnp.where(mask, w_s, 0)
    mixed = jnp.einsum('gts,bnsgc->bntgc', w, vg) + b_s.T[:, :, None]
    return u * mixed.reshape(b, s, GM_WIDTH)


def stick_breaking_attention(q, k, v):
    b, s, h, d = q.shape
    nq = s // SB_QBLOCK
    qf = (q.astype(jnp.float32) * (d ** -0.5)).transpose(0, 2, 1, 3)
    kf = k.astype(jnp.float32).transpose(0, 2, 1, 3)
    vf = v.astype(jnp.float32).transpose(0, 2, 1, 3)
    q_blocks = qf.reshape(b, h, nq, SB_QBLOCK, d).transpose(2, 0, 1, 3, 4)
    key_pos = jnp.arange(s)

    def block(args):
        qb, i = args
        q_pos = i * SB_QBLOCK + jnp.arange(SB_QBLOCK)
        mask = key_pos[None, :] < q_pos[:, None]
        z = jnp.einsum('bhqd,bhkd->bhqk', qb, kf)
        log_beta = jax.nn.log_sigmoid(z)
        log_stay = jnp.where(mask, log_beta - z, 0.0)
        after = lax.cumsum(log_stay, axis=3, reverse=True) - log_stay
        a = jnp.where(mask, jnp.exp(log_beta + after), 0.0)
        return jnp.einsum('bhqk,bhkd->bhqd', a, vf)

    o = lax.map(block, (q_blocks, jnp.arange(nq)))
    return o.transpose(1, 0, 3, 2, 4).reshape(b, s, h * d).astype(q.dtype)


def memory_cross_attention(h, mem, w_q, w_k, w_v, w_o):
    b, s, dm = h.shape
    m = mem.shape[1]
    q = (h @ w_q).reshape(b, s, MEM_HEADS, MEM_HEAD_DIM).astype(jnp.float32)
    k = (mem @ w_k).reshape(b, m, MEM_HEADS, MEM_HEAD_DIM).astype(jnp.float32)
    v = (mem @ w_v).reshape(b, m, MEM_HEADS, MEM_HEAD_DIM).astype(jnp.float32)
    scores = jnp.einsum('bshd,bmhd->bhsm', q, k) * (MEM_HEAD_DIM ** -0.5)
    p = jax.nn.softmax(scores, axis=-1)
    o = jnp.einsum('bhsm,bmhd->bshd', p, v).astype(h.dtype).reshape(b, s, dm)
    return o @ w_o


def hierarchical_moe(h, w_group, b_group, w_router, b_router, w1, w3, w2):
    b, s, dm = h.shape
    t = b * s
    xf = h.reshape(t, dm)
    g_logits = (xf @ w_group).astype(jnp.float32) + b_group.astype(jnp.float32)
    g_probs = jax.nn.softmax(g_logits, axis=-1)
    g_val, g_idx = lax.top_k(g_probs, 1)
    e_logits_all = jnp.einsum('td,gde->tge', xf, w_router).astype(jnp.float32) + b_router.astype(jnp.float32)
    e_logits = jnp.take_along_axis(e_logits_all, g_idx[:, :, None], axis=1)[:, 0]
    top_val, top_idx = lax.top_k(e_logits, TOP_K)
    gate = jax.nn.softmax(top_val, axis=-1) * g_val
    expert_id = g_idx * EXPERTS_PER_GROUP + top_idx

    n_assign = t * TOP_K
    flat_e = expert_id.reshape(-1)
    flat_tok = jnp.repeat(jnp.arange(t, dtype=jnp.int32), TOP_K)
    flat_gate = gate.reshape(-1)
    order = jnp.argsort(flat_e)
    se, stok, sgate = flat_e[order], flat_tok[order], flat_gate[order]
    counts = jnp.bincount(flat_e, length=N_EXPERTS)
    starts = jnp.cumsum(counts) - counts
    padded = (counts + MOE_BLOCK - 1) // MOE_BLOCK * MOE_BLOCK
    pad_ends = jnp.cumsum(padded)
    pad_starts = pad_ends - padded
    dest = pad_starts[se] + jnp.arange(n_assign) - starts[se]
    n_blocks = -(-n_assign // MOE_BLOCK) + N_EXPERTS
    n_rows = n_blocks * MOE_BLOCK
    row_tok = jnp.full((n_rows,), t, jnp.int32).at[dest].set(stok)
    row_gate = jnp.zeros((n_rows,), jnp.float32).at[dest].set(sgate)
    block_expert = jnp.minimum(
        jnp.searchsorted(pad_ends, jnp.arange(n_blocks) * MOE_BLOCK, side='right'), N_EXPERTS - 1)

    x_pad = jnp.concatenate([xf, jnp.zeros((1, dm), xf.dtype)], axis=0)
    x_rows = x_pad[row_tok].reshape(n_blocks, MOE_BLOCK, dm)

    def expert_block(args):
        xb, e = args
        hid = jax.nn.silu(xb @ w1[e]) * (xb @ w3[e])
        return hid @ w2[e]

    y_rows = lax.map(expert_block, (x_rows, block_expert)).reshape(n_rows, dm)
    y = jnp.zeros((t + 1, dm), jnp.float32).at[row_tok].add(
        y_rows.astype(jnp.float32) * row_gate[:, None])[:t]
    return y.astype(h.dtype).reshape(b, s, dm)


def setup_inputs(seed: int = 0) -> dict:
    key = jax.random.key(seed)
    ks = jax.random.split(key, 26)
    f32 = jnp.float32
    nrm = lambda k, shape, scale: jax.random.normal(k, shape, f32) * scale
    col_scale = jnp.concatenate([
        jnp.full((2 * GM_WIDTH,), DN_BETA, f32),
        jnp.ones((2 * SB_WIDTH,), f32),
        jnp.full((SB_WIDTH,), DN_BETA, f32)])
    return {
        "x": nrm(ks[0], (BATCH, SEQ, D_MODEL), 1.0),
        "mem": nrm(ks[1], (BATCH, MEM_LEN, D_MODEL), 1.0),
        "w_in": nrm(ks[2], (DEPTH, D_MODEL, IN_WIDTH), D_MODEL ** -0.5) * col_scale,
        "gm_ln_g": 1.0 + nrm(ks[3], (DEPTH, GM_GROUPS, GM_GROUP_DIM), 0.02),
        "gm_ln_b": nrm(ks[4], (DEPTH, GM_GROUPS, GM_GROUP_DIM), 0.02),
        "gm_w_s": nrm(ks[5], (DEPTH, GM_GROUPS, GM_BLOCK, GM_BLOCK), GM_BLOCK ** -0.5),
        "gm_b_s": 1.0 + nrm(ks[6], (DEPTH, GM_GROUPS, GM_BLOCK), 0.01),
        "w_mix_out": nrm(ks[7], (DEPTH, MIX_WIDTH, D_MODEL), MIX_WIDTH ** -0.5 * DN_BETA),
        "ln1_g": 1.0 + nrm(ks[8], (DEPTH, D_MODEL), 0.02),
        "ln1_b": nrm(ks[9], (DEPTH, D_MODEL), 0.02),
        "mem_w_q": nrm(ks[10], (DEPTH, D_MODEL, D_MODEL), D_MODEL ** -0.5),
        "mem_w_k": nrm(ks[11], (DEPTH, D_MODEL, D_MODEL), D_MODEL ** -0.5),
        "mem_w_v": nrm(ks[12], (DEPTH, D_MODEL, D_MODEL), D_MODEL ** -0.5 * DN_BETA),
        "mem_w_o": nrm(ks[13], (DEPTH, D_MODEL, D_MODEL), D_MODEL ** -0.5 * DN_BETA),
        "ln2_g": 1.0 + nrm(ks[14], (DEPTH, D_MODEL), 0.02),
        "ln2_b": nrm(ks[15], (DEPTH, D_MODEL), 0.02),
        "w_group": nrm(ks[16], (DEPTH, D_MODEL, N_GROUPS), D_MODEL ** -0.5),
        "b_group": nrm(ks[17], (DEPTH, N_GROUPS), 0.01),
        "w_router": nrm(ks[18], (DEPTH, N_GROUPS, D_MODEL, EXPERTS_PER_GROUP), D_MODEL ** -0.5),
        "b_router": nrm(ks[19], (DEPTH, N_GROUPS, EXPERTS_PER_GROUP), 0.01),
        "w1": nrm(ks[20], (DEPTH, N_EXPERTS, D_MODEL, D_EXPERT), D_MODEL ** -0.5 * DN_BETA),
        "w3": nrm(ks[21], (DEPTH, N_EXPERTS, D_MODEL, D_EXPERT), D_MODEL ** -0.5 * DN_BETA),
        "w2": nrm(ks[22], (DEPTH, N_EXPERTS, D_EXPERT, D_MODEL), D_EXPERT ** -0.5 * DN_BETA),
        "ln3_g": 1.0 + nrm(ks[23], (DEPTH, D_MODEL), 0.02),
        "ln3_b": nrm(ks[24], (DEPTH, D_MODEL), 0.02),
    }


def reference(x, mem, w_in, gm_ln_g, gm_ln_b, gm_w_s, gm_b_s, w_mix_out, ln1_g, ln1_b,
              mem_w_q, mem_w_k, mem_w_v, mem_w_o, ln2_g, ln2_b,
              w_group, b_group, w_router, b_router, w1, w3, w2, ln3_g, ln3_b):
    b, s, dm = x.shape
    h = x
    for l in range(DEPTH):
        proj = h @ w_in[l]
        u_a, v_a, q, k, v = jnp.split(
            proj, [GM_WIDTH, 2 * GM_WIDTH, 2 * GM_WIDTH + SB_WIDTH, 2 * GM_WIDTH + 2 * SB_WIDTH], axis=-1)
        mix_a = chunked_spatial_gating(jax.nn.gelu(u_a), jax.nn.gelu(v_a),
                                       gm_ln_g[l], gm_ln_b[l], gm_w_s[l], gm_b_s[l])
        mix_b = stick_breaking_attention(q.reshape(b, s, SB_HEADS, SB_HEAD_DIM),
                                         k.reshape(b, s, SB_HEADS, SB_HEAD_DIM),
                                         v.reshape(b, s, SB_HEADS, SB_HEAD_DIM))
        mixed = jnp.concatenate([mix_a, mix_b], axis=-1) @ w_mix_out[l]
        h = layer_norm(DN_ALPHA * h + mixed, ln1_g[l], ln1_b[l])
        h = layer_norm(DN_ALPHA * h + memory_cross_attention(h, mem, mem_w_q[l], mem_w_k[l],
                                                             mem_w_v[l], mem_w_o[l]),
                       ln2_g[l], ln2_b[l])
        h = layer_norm(DN_ALPHA * h + hierarchical_moe(h, w_group[l], b_group[l], w_router[l],
                                                       b_router[l], w1[l], w3[l], w2[l]),
                       ln3_g[l], ln3_b[l])
    return h
```

```python
import numpy as np
from contextlib import ExitStack
import ml_dtypes
import concourse.bass as bass
import concourse.mybir as mybir
from concourse.bass_utils import run_bass_kernel_spmd

F32 = mybir.dt.float32
BF16 = mybir.dt.bfloat16
ALU = mybir.AluOpType
AF = mybir.ActivationFunctionType
AX = mybir.AxisListType

NCORES = 8
S = 8192
D = 2048
TPC = S // NCORES
NTC = TPC // 128
MEM = 256
DN_ALPHA = 2.0 ** 0.25
LN_EPS = 1e-5
GELU_C = 1.5957691216057308
CAP = 256
NSB = CAP // 128


class Reg:
    __slots__ = ("w", "r")

    def __init__(self):
        self.w = None
        self.r = {}


class Sched:
    NDMA = 12

    def __init__(self, nc):
        self.nc = nc
        self.eng = {"pe": nc.tensor, "act": nc.scalar, "dve": nc.vector,
                    "pool": nc.gpsimd, "sp": nc.sync}
        self.sems, self.cnt, self.waited = {}, {}, {}
        for e in ("pe", "act", "dve", "pool"):
            self.sems[e] = nc.alloc_semaphore(name="s_" + e)
            self.cnt[e] = 0
        self.dma_i = {"sp": 0, "pool": 0, "act": 0}
        for q in ("sp", "pool", "act"):
            for i in range(self.NDMA):
                k = "d_%s_%d" % (q, i)
                self.sems[k] = nc.alloc_semaphore(name=k)
                self.cnt[k] = 0
        self.cc_n = 0
        self.n_inst = 0
        self.n_wait = 0

    def _wait(self, e, dep):
        key, val = dep
        if self.waited.get((e, key), 0) >= val:
            return
        self.eng[e].wait_ge(self.sems[key], val)
        self.waited[(e, key)] = val
        self.n_wait += 1

    def _deps(self, e, reads, writes):
        deps = {}

        def add(d):
            if d is None:
                return
            k, v = d
            if k == e and e == "pe":
                return
            if deps.get(k, 0) < v:
                deps[k] = v
        for t in reads:
            add(t.w)
        for t in writes:
            add(t.w)
            for k, v in t.r.items():
                if k != e:
                    add((k, v))
        for k, v in deps.items():
            self._wait(e, (k, v))

    def _mark(self, key, val, reads, writes):
        for t in reads:
            if t.r.get(key, 0) < val:
                t.r[key] = val
        for t in writes:
            t.w = (key, val)
            t.r = {}

    def op(self, e, fn, reads=(), writes=()):
        self._deps(e, reads, writes)
        ins = fn(self.eng[e])
        self.cnt[e] += 1
        ins.then_inc(self.sems[e], 1)
        self._mark(e, self.cnt[e], reads, writes)
        self.n_inst += 1
        return ins

    def dma(self, q, out, in_, reads=(), writes=(), **kw):
        i = self.dma_i[q]
        self.dma_i[q] = i + 1
        key = "d_%s_%d" % (q, i % self.NDMA)
        if self.cnt[key] > 0:
            self._wait(q, (key, self.cnt[key]))
        self._deps(q, reads, writes)
        ins = self.eng[q].dma_start(out=out, in_=in_, **kw)
        self.cnt[key] += 16
        ins.then_inc(self.sems[key], 16)
        self._mark(key, self.cnt[key], reads, writes)
        self.n_inst += 1
        return ins

    def collective(self, kind, alu, in_ap, out_ap, reads=(), writes=()):
        self._deps("pool", reads, writes)
        ins = self.nc.gpsimd.collective_compute(
            kind, alu, replica_groups=[list(range(NCORES))],
            ins=[in_ap.opt()], outs=[out_ap.opt()])
        key = "cc%d" % self.cc_n
        self.cc_n += 1
        self.sems[key] = self.nc.alloc_semaphore(name=key)
        ins.then_inc(self.sems[key])
        self._mark(key, 1, reads, writes)
        return ins

    def allgather(self, in_ap, out_ap, reads=(), writes=(), bounce=None):
        if bounce is not None:
            rb = Reg()
            self.dma("sp", bounce, in_ap, reads=reads, writes=[rb])
            in_ap, reads = bounce, [rb]
        self._deps("pool", reads, writes)
        ins = self.nc.gpsimd.collective_compute(
            "AllGather", ALU.bypass, replica_groups=[list(range(NCORES))],
            ins=[in_ap.opt()], outs=[out_ap.opt()])
        key = "cc%d" % self.cc_n
        self.cc_n += 1
        self.sems[key] = self.nc.alloc_semaphore(name=key)
        ins.then_inc(self.sems[key])
        self._mark(key, 1, reads, writes)
        return ins

    def barrier(self):
        for e in ("pe", "act", "dve", "pool", "sp"):
            for k, v in self.cnt.items():
                if v > 0 and k != e:
                    self._wait(e, (k, v))

    def wait_all(self, e, regs):
        for t in regs:
            if t.w is not None:
                self._wait(e, t.w)


class Ring:
    def __init__(self, bufs):
        self.bufs = [(b, Reg()) for b in bufs]
        self.i = 0

    def next(self):
        b = self.bufs[self.i % len(self.bufs)]
        self.i += 1
        return b


def build(debug=None):
    nc = bass.Bass("TRN2", target_bir_lowering=False)
    SC = Sched(nc)
    op, dma = SC.op, SC.dma

    def din(name, shape, dt=F32):
        return nc.dram_tensor(name, shape, dt, kind="ExternalInput")

    G = {}
    xT_sh = din("xT_sh", [D // NCORES, S])
    G["w_in_c"] = din("w_in_c", [D, 640])
    G["gm_par"] = din("gm_par", [1, 3 * 128])
    G["gm_wT"] = din("gm_wT", [128, 128])
    G["wmo_c"] = din("wmo_c", [256, D])
    xT_full = nc.dram_tensor("xT_full", [D, S], BF16)
    G["xT_full"] = xT_full
    G["part"] = nc.dram_tensor("part", [S, D], F32)
    G["mixed"] = nc.dram_tensor("mixed", [TPC, D], F32)
    R = {k: Reg() for k in ("xT", "part", "mixed", "wq", "wk", "wv", "wo", "w1", "w3", "w2", "out")}
    G["R"] = R
    G["r_part"] = [Reg() for _ in range(S // 128)]
    if debug == "p1":
        out = nc.dram_tensor("out", [TPC, D], F32, kind="ExternalOutput")
    else:
        out = nc.dram_tensor("out", [TPC, D], F32, kind="ExternalOutput")
        G["x_tok"] = din("x_tok", [TPC, D])
        wq_sh = din("wq_sh", [D // NCORES, D])
        wk_sh = din("wk_sh", [D // NCORES, D])
        wv_sh = din("wv_sh", [D // NCORES, D])
        wo_sh = din("wo_sh", [D // NCORES, D])
        G["memT"] = din("memT", [D, MEM])
        G["ln_par"] = din("ln_par", [1, 6 * D])
        G["w_rt"] = din("w_rt", [D, 72])
        G["b_rt"] = din("b_rt", [1, 72])
        if debug != "p2":
            w1_sh = din("w1_sh", [8 * D, 512])
            w3_sh = din("w3_sh", [8 * D, 512])
            w2_sh = din("w2_sh", [8 * 512, D])
            G["w1_f"] = nc.dram_tensor("w1_f", [64 * D, 512], BF16)
            G["w3_f"] = nc.dram_tensor("w3_f", [64 * D, 512], BF16)
            G["w2_f"] = nc.dram_tensor("w2_f", [64 * 512, D], BF16)
        for n in ("wq", "wk", "wv", "wo"):
            G[n + "_f"] = nc.dram_tensor(n + "_f", [D, D], BF16)
    G["out"] = out

    ident = nc.alloc_sbuf_tensor("ident", [128, 128], BF16)
    ones_f = nc.alloc_sbuf_tensor("ones_f", [128, 512], F32)
    r_ident, r_ones = Reg(), Reg()
    op("dve", lambda e: e.memset(ones_f[:], 1.0), writes=[r_ones])
    op("pool", lambda e: e.affine_select(out=ident[:], in_=ones_f[:, 0:128], pattern=[[1, 128]],
                                         compare_op=ALU.is_equal, fill=0.0, base=0,
                                         channel_multiplier=-1),
       reads=[r_ones], writes=[r_ident])
    G.update(ident=ident, r_ident=r_ident, ones_f=ones_f, r_ones=r_ones)

    es_stage = ExitStack()
    stg_ring = Ring([es_stage.enter_context(nc.sbuf_tensor("cstg%d" % i, [128, 4, 2048], BF16))
                     for i in range(3)])
    bg = []
    bg_late = []

    def ag_in(src, dst, r, now=False, late=False):
        rows, cols = src.shape
        b = nc.dram_tensor(src.name + "_b", [rows, cols], BF16)
        n = rows * cols // 2048
        if cols >= 2048:
            sv = src.ap().rearrange("r (a e) -> (r a) e", e=2048)
            bv = b.ap().rearrange("r (a e) -> (r a) e", e=2048)
        else:
            sv = src.ap().rearrange("(r a) c -> r (a c)", a=2048 // cols)
            bv = b.ap().rearrange("(r a) c -> r (a c)", a=2048 // cols)
        regs = []
        tasks = []
        for i in range(0, n, 512):
            m = min(512, n - i) // 128

            def piece(i=i, m=m):
                st, r_st = stg_ring.next()
                rb = Reg()
                dma("pool", st[:, 0:m, :], sv[i:i + m * 128, :].rearrange("(t p) e -> p t e", p=128),
                    writes=[r_st])
                dma("sp", bv[i:i + m * 128, :].rearrange("(t p) e -> p t e", p=128), st[:, 0:m, :],
                    reads=[r_st], writes=[rb])
                regs.append(rb)
            tasks.append(piece)
        ag_task = lambda: SC.allgather(b.ap(), dst.ap(), reads=regs, writes=[r])
        if late:
            bg_late.append(ag_task)
        else:
            tasks.append(ag_task)
        if now:
            for t in tasks:
                t()
        else:
            bg.extend(tasks)

    def bg_step(n=1):
        for _ in range(n):
            if bg:
                bg.pop(0)()
    G["bg_step"] = bg_step

    ag_in(xT_sh, xT_full, R["xT"], now=True)
    if debug != "p1":
        ag_in(wq_sh, G["wq_f"], R["wq"])
        ag_in(wk_sh, G["wk_f"], R["wk"])
        ag_in(wv_sh, G["wv_f"], R["wv"])
        ag_in(wo_sh, G["wo_f"], R["wo"])
        if debug != "p2":
            ag_in(w1_sh, G["w1_f"], R["w1"])
            ag_in(w3_sh, G["w3_f"], R["w3"], late=True)
            ag_in(w2_sh, G["w2_f"], R["w2"], late=True)

    phase1(nc, SC, G)
    bg_step(len(bg))
    SC.barrier()
    es_stage.close()
    SC.collective("ReduceScatter", ALU.add, G["part"].ap(), G["mixed"].ap(),
                  reads=G["r_part"], writes=[R["mixed"]])
    for t in bg_late:
        t()
    if debug == "p1":
        with nc.sbuf_tensor("dbg", [128, NTC, D], F32) as dbg:
            r = Reg()
            dma("sp", dbg[:], G["mixed"].ap().rearrange("(a p) d -> p a d", p=128), reads=[R["mixed"]],
                writes=[r])
            dma("sp", out.ap().rearrange("(a p) d -> p a d", p=128), dbg[:], reads=[r], writes=[R["out"]])
        SC.wait_all("sp", [R["out"]])
        return nc

    phase23(nc, SC, G, debug)
    SC.wait_all("sp", [R["out"]])
    print("instructions", SC.n_inst, "waits", SC.n_wait)
    return nc


def phase1(nc, SC, G):
    op, dma = SC.op, SC.dma
    xT_full, w_in_c, gm_par, gm_wT = G["xT_full"], G["w_in_c"], G["gm_par"], G["gm_wT"]
    ident, r_ident, ones_f, r_ones = G["ident"], G["r_ident"], G["ones_f"], G["r_ones"]
    r_xT = G["R"]["xT"]
    NCH = S // 512
    with ExitStack() as _es:
        w_sb = _es.enter_context(nc.sbuf_tensor("w_sb", [128, 16, 640], BF16))
        QT = _es.enter_context(nc.sbuf_tensor("QT", [128, S], BF16))
        KT = _es.enter_context(nc.sbuf_tensor("KT", [128, S], BF16))
        Vt = _es.enter_context(nc.sbuf_tensor("Vt", [128, S // 128, 128], BF16))
        mixA = _es.enter_context(nc.sbuf_tensor("mixA", [128, S], BF16))
        mixB = _es.enter_context(nc.sbuf_tensor("mixB", [128, S], BF16))
        gpar = _es.enter_context(nc.sbuf_tensor("gpar", [128, 384], F32))
        wsT = _es.enter_context(nc.sbuf_tensor("wsT", [128, 128], BF16))
        neghalf = _es.enter_context(nc.sbuf_tensor("neghalf", [128, 1], F32))
        r_w, r_gpar, r_wsT, r_nh = Reg(), Reg(), Reg(), Reg()
        r_Q = [Reg() for _ in range(NCH)]
        r_K = [Reg() for _ in range(NCH)]
        r_V = [Reg() for _ in range(NCH)]
        r_mA = [Reg() for _ in range(NCH)]
        r_mB = [Reg() for _ in range(S // 128)]
        dma("pool", w_sb[:], w_in_c.ap().rearrange("(dc p) n -> p dc n", p=128), writes=[r_w])
        dma("sp", gpar[:], gm_par.ap().partition_broadcast(128)[:, 0, :], writes=[r_gpar])
        dma("pool", wsT[:], gm_wT.ap(), writes=[r_wsT])
        op("dve", lambda e: e.memset(wsT[0:64, 64:128], 0.0), writes=[r_wsT])
        op("dve", lambda e: e.memset(neghalf[:], -0.5), writes=[r_nh])

        with ExitStack() as _es:
            xb0 = _es.enter_context(nc.sbuf_tensor("xb0", [128, 16, 512], BF16))
            xb1 = _es.enter_context(nc.sbuf_tensor("xb1", [128, 16, 512], BF16))
            g_x = _es.enter_context(nc.sbuf_tensor("g_x", [128, 2, 512], F32))
            g_t = _es.enter_context(nc.sbuf_tensor("g_t", [128, 2, 512], F32))
            g_s = _es.enter_context(nc.sbuf_tensor("g_s", [128, 2, 512], F32))
            UT = _es.enter_context(nc.sbuf_tensor("UT", [128, 2, 512], F32))
            v_x = _es.enter_context(nc.sbuf_tensor("v_x", [128, 2, 128], F32))
            v_t = _es.enter_context(nc.sbuf_tensor("v_t", [128, 2, 128], F32))
            v_s = _es.enter_context(nc.sbuf_tensor("v_s", [128, 2, 128], F32))
            v_g = _es.enter_context(nc.sbuf_tensor("v_g", [128, 2, 128], F32))
            v_n = _es.enter_context(nc.sbuf_tensor("v_n", [128, 2, 128], BF16))
            st6 = _es.enter_context(nc.sbuf_tensor("st6", [128, 2, 6], F32))
            st2 = _es.enter_context(nc.sbuf_tensor("st2", [128, 2, 2], F32))
            rstd = _es.enter_context(nc.sbuf_tensor("rstd", [128, 2, 1], F32))
            m_t = _es.enter_context(nc.sbuf_tensor("m_t", [128, 2, 128], F32))
            pp = _es.enter_context(nc.psum_tensor("pp", [128, 8, 512], F32))
            xring = Ring([xb0, xb1])
            pring = Ring([pp[:, i, :] for i in range(8)])
            rr = {n: [Reg(), Reg()] for n in
                  ("g_x", "g_t", "g_s", "UT", "v_x", "v_t", "v_s", "v_g", "v_n", "st6", "st2",
                   "rstd", "m_t")}
            xTv = xT_full.ap().rearrange("(dc p) t -> p dc t", p=128)
            gi = 0
            vi = 0

            def gelu(src_ps, r_ps, width, xs, ts, ss, dst, rxs, rts, rss, rdst):
                op("act", lambda e: e.activation(out=xs, in_=src_ps, func=AF.Copy),
                   reads=[r_ps], writes=[rxs])
                op("act", lambda e: e.activation(out=ts, in_=src_ps, func=AF.Square),
                   reads=[r_ps], writes=[rts])
                op("dve", lambda e: e.tensor_scalar(out=ts, in0=ts, scalar1=0.044715, scalar2=1.0,
                                                    op0=ALU.mult, op1=ALU.add),
                   reads=[rts], writes=[rts])
                op("dve", lambda e: e.tensor_tensor(out=ts, in0=ts, in1=xs, op=ALU.mult),
                   reads=[rts, rxs], writes=[rts])
                op("act", lambda e: e.activation(out=ss, in_=ts, func=AF.Sigmoid, scale=GELU_C),
                   reads=[rts], writes=[rss])
                op("dve", lambda e: e.tensor_tensor(out=dst, in0=ss, in1=xs, op=ALU.mult),
                   reads=[rss, rxs], writes=[rdst])

            for j in range(NCH):
                xb, r_xb = xring.next()
                tok = slice(j * 512, (j + 1) * 512)
                dma("sp", xb[:], xTv[:, :, tok], reads=[r_xT], writes=[r_xb])
                for which in range(3):
                    ps, r_ps = pring.next()
                    for dc in range(16):
                        op("pe", lambda e, dc=dc, ps=ps, which=which: e.matmul(
                            ps, lhsT=w_sb[:, dc, which * 128:(which + 1) * 128], rhs=xb[:, dc, :],
                            start=(dc == 0), stop=(dc == 15)),
                           reads=[r_w, r_xb], writes=[r_ps])
                    if which == 0:
                        op("act", lambda e, ps=ps: e.activation(out=QT[:, tok], in_=ps, func=AF.Copy,
                                                                 scale=128.0 ** -0.5),
                           reads=[r_ps], writes=[r_Q[j]])
                    elif which == 1:
                        op("dve", lambda e, ps=ps: e.tensor_copy(out=KT[:, tok], in_=ps),
                           reads=[r_ps], writes=[r_K[j]])
                    else:
                        g = gi % 2
                        gi += 1
                        gelu(ps, r_ps, 512, g_x[:, g, :], g_t[:, g, :], g_s[:, g, :], UT[:, g, :],
                             rr["g_x"][g], rr["g_t"][g], rr["g_s"][g], rr["UT"][g])
                        ug = g
                for b in range(4):
                    blk = j * 4 + b
                    ps, r_ps = pring.next()
                    for dc in range(16):
                        op("pe", lambda e, dc=dc, ps=ps, b=b: e.matmul(
                            ps[:, 0:256], lhsT=xb[:, dc, b * 128:(b + 1) * 128], rhs=w_sb[:, dc, 384:640],
                            start=(dc == 0), stop=(dc == 15)),
                           reads=[r_w, r_xb], writes=[r_ps])
                    op("act", lambda e, ps=ps, blk=blk: e.activation(out=Vt[:, blk, :], in_=ps[:, 0:128],
                                                                     func=AF.Copy),
                       reads=[r_ps], writes=[r_V[j]])
                    v = vi % 2
                    vi += 1
                    gelu(ps[:, 128:256], r_ps, 128, v_x[:, v, :], v_t[:, v, :], v_s[:, v, :], v_g[:, v, :],
                         rr["v_x"][v], rr["v_t"][v], rr["v_s"][v], rr["v_g"][v])
                    op("dve", lambda e, v=v: e.bn_stats(out=st6[:, v, :], in_=v_g[:, v, :]),
                       reads=[rr["v_g"][v]], writes=[rr["st6"][v]])
                    op("dve", lambda e, v=v: e.bn_aggr(out=st2[:, v, :], in_=st6[:, v, :]),
                       reads=[rr["st6"][v]], writes=[rr["st2"][v]])
                    op("dve", lambda e, v=v: e.tensor_scalar(out=rstd[:, v, :], in0=st2[:, v, 1:2],
                                                             scalar1=LN_EPS, scalar2=None, op0=ALU.add),
                       reads=[rr["st2"][v]], writes=[rr["rstd"][v]])
                    op("pool", lambda e, v=v: e.tensor_tensor(out=rstd[:, v, :], in0=rstd[:, v, :],
                                                              in1=neghalf[:], op=ALU.pow),
                       reads=[rr["rstd"][v], r_nh], writes=[rr["rstd"][v]])
                    op("dve", lambda e, v=v: e.tensor_scalar(out=v_t[:, v, :], in0=v_g[:, v, :],
                                                             scalar1=st2[:, v, 0:1], scalar2=rstd[:, v, :],
                                                             op0=ALU.subtract, op1=ALU.mult),
                       reads=[rr["v_g"][v], rr["st2"][v], rr["rstd"][v]], writes=[rr["v_t"][v]])
                    op("pool", lambda e, v=v: e.tensor_tensor(out=v_t[:, v, :], in0=v_t[:, v, :],
                                                              in1=gpar[:, 0:128], op=ALU.mult),
                       reads=[rr["v_t"][v], r_gpar], writes=[rr["v_t"][v]])
                    op("pool", lambda e, v=v: e.tensor_tensor(out=v_n[:, v, :], in0=v_t[:, v, :],
                                                              in1=gpar[:, 128:256], op=ALU.add),
                       reads=[rr["v_t"][v], r_gpar], writes=[rr["v_n"][v]])
                    ps2, r_ps2 = pring.next()
                    op("pe", lambda e, v=v, ps2=ps2: e.matmul(ps2[:, 0:128], lhsT=v_n[:, v, :], rhs=wsT[:],
                                                              start=True, stop=True),
                       reads=[rr["v_n"][v], r_wsT], writes=[r_ps2])
                    op("dve", lambda e, v=v, ps2=ps2: e.tensor_tensor(out=m_t[:, v, :], in0=ps2[:, 0:128],
                                                                      in1=gpar[:, 256:384], op=ALU.add),
                       reads=[r_ps2, r_gpar], writes=[rr["m_t"][v]])
                    op("dve", lambda e, v=v, b=b, blk=blk, ug=ug: e.tensor_tensor(
                        out=mixA[:, blk * 128:(blk + 1) * 128], in0=m_t[:, v, :],
                        in1=UT[:, ug, b * 128:(b + 1) * 128], op=ALU.mult),
                       reads=[rr["m_t"][v], rr["UT"][ug]], writes=[r_mA[j]])

        SC.barrier()
        with ExitStack() as _es:
            e_sb = _es.enter_context(nc.sbuf_tensor("e_sb", [128, 4, 512], F32))
            L_sb = _es.enter_context(nc.sbuf_tensor("L_sb", [128, 4, 512], F32))
            C_sb = _es.enter_context(nc.sbuf_tensor("C_sb", [128, 4, 512], F32))
            c_sb = _es.enter_context(nc.sbuf_tensor("c_sb", [128, 4, 512], F32))
            A_sb = _es.enter_context(nc.sbuf_tensor("A_sb", [128, 4, 512], BF16))
            AT_sb = _es.enter_context(nc.sbuf_tensor("AT_sb", [128, 2, 512], BF16))
            zp = _es.enter_context(nc.psum_tensor("zp", [128, 3, 512], F32))
            atp = _es.enter_context(nc.psum_tensor("atp", [128, 2, 1024], BF16))
            otp = _es.enter_context(nc.psum_tensor("otp", [128, 2, 512], F32))
            NB = 4
            rz = [Reg() for _ in range(3)]
            re_ = [Reg() for _ in range(NB)]
            rL = [Reg() for _ in range(NB)]
            rC = [Reg() for _ in range(NB)]
            rc = [Reg() for _ in range(NB)]
            rA = [Reg() for _ in range(NB)]
            ratp = [Reg(), Reg()]
            rATs = [Reg(), Reg()]
            rotp = [Reg(), Reg()]
            units = []
            for i in range(S // 128):
                kb = i
                first = True
                while kb < S // 128:
                    n = min(4 - (kb % 4), S // 128 - kb)
                    units.append((i, kb, n, first, kb + n == S // 128))
                    first = False
                    kb += n

            def stage_z(u, k):
                i, kb, n, first, last = units[u]
                W = n * 128
                jq = i // 4
                kregs = [r_K[c] for c in sorted(set((kb + t) // 4 for t in range(n)))]
                op("pe", lambda e: e.matmul(zp[:, k, 0:W], lhsT=QT[:, i * 128:(i + 1) * 128],
                                            rhs=KT[:, kb * 128:kb * 128 + W], start=True, stop=True),
                   reads=[r_Q[jq]] + kregs, writes=[rz[k]])

            def stage_a1(u, k, kz):
                i, kb, n, first, last = units[u]
                W = n * 128
                op("act", lambda e: e.activation(out=e_sb[:, k, 0:W], in_=zp[:, kz, 0:W], func=AF.Exp),
                   reads=[rz[kz]], writes=[re_[k]])
                op("act", lambda e: e.activation(out=L_sb[:, k, 0:W], in_=e_sb[:, k, 0:W], func=AF.Ln,
                                                 bias=1.0),
                   reads=[re_[k]], writes=[rL[k]])
                if first:
                    for buf, r in ((L_sb, rL[k]), (e_sb, re_[k])):
                        op("pool", lambda e, buf=buf: e.affine_select(
                            out=buf[:, k, 0:128], in_=buf[:, k, 0:128], pattern=[[1, 128]],
                            compare_op=ALU.is_gt, fill=0.0, base=0, channel_multiplier=-1),
                           reads=[r], writes=[r])

            def stage_a2(u, k, kprev):
                i, kb, n, first, last = units[u]
                W = n * 128
                if first:
                    init = 0.0
                    extra = []
                else:
                    pi, pkb, pn, _, _ = units[u - 1]
                    init = C_sb[:, kprev, pn * 128 - 1:pn * 128]
                    extra = [rC[kprev]]
                op("dve", lambda e: e.tensor_tensor_scan(out=C_sb[:, k, 0:W], data0=ones_f[:, 0:W],
                                                         data1=L_sb[:, k, 0:W], initial=init,
                                                         op0=ALU.mult, op1=ALU.add),
                   reads=[rL[k], r_ones] + extra, writes=[rC[k]])
                op("act", lambda e: e.activation(out=c_sb[:, k, 0:W], in_=C_sb[:, k, 0:W], func=AF.Exp,
                                                 scale=-1.0),
                   reads=[rC[k]], writes=[rc[k]])
                op("pool", lambda e: e.tensor_tensor(out=A_sb[:, k, 0:W], in0=e_sb[:, k, 0:W],
                                                     in1=c_sb[:, k, 0:W], op=ALU.mult),
                   reads=[re_[k], rc[k]], writes=[rA[k]])

            def stage_b(u, k):
                i, kb, n, first, last = units[u]
                W = n * 128
                a = u % 2
                o = i % 2
                for t in range(n):
                    op("pe", lambda e, t=t: e.transpose(out=atp[:, a, t * 128:(t + 1) * 128],
                                                        in_=A_sb[:, k, t * 128:(t + 1) * 128],
                                                        identity=ident[:]),
                       reads=[rA[k], r_ident], writes=[ratp[a]])
                op("dve", lambda e: e.tensor_copy(out=AT_sb[:, a, 0:W], in_=atp[:, a, 0:W]),
                   reads=[ratp[a]], writes=[rATs[a]])
                for t in range(n):
                    op("pe", lambda e, t=t: e.matmul(otp[:, o, 0:128], lhsT=Vt[:, kb + t, :],
                                                     rhs=AT_sb[:, a, t * 128:(t + 1) * 128],
                                                     start=(first and t == 0), stop=(last and t == n - 1)),
                       reads=[r_V[(kb + t) // 4], rATs[a]], writes=[rotp[o]])
                if last:
                    op("act", lambda e: e.activation(out=mixB[:, i * 128:(i + 1) * 128], in_=otp[:, o, 0:128],
                                                     func=AF.Copy),
                       reads=[rotp[o]], writes=[r_mB[i]])

            NU = len(units)
            stage_z(0, 0)
            if NU > 1:
                stage_z(1, 1)
            stage_a1(0, 0, 0)
            for u in range(NU):
                if u % 8 == 0:
                    G["bg_step"]()
                if u + 2 < NU:
                    stage_z(u + 2, (u + 2) % 3)
                if u + 1 < NU:
                    stage_a1(u + 1, (u + 1) % NB, (u + 1) % 3)
                stage_a2(u, u % NB, (u - 1) % NB)
                if u >= 1:
                    stage_b(u - 1, (u - 1) % NB)
            stage_b(NU - 1, (NU - 1) % NB)

        SC.barrier()
        with ExitStack() as _es:
            wmo_sb = _es.enter_context(nc.sbuf_tensor("wmo_sb", [128, 2, D], BF16))
            stg = _es.enter_context(nc.sbuf_tensor("stg", [128, 3, D], F32))
            pq = _es.enter_context(nc.psum_tensor("pq", [128, 8, 512], F32))
            r_wmo = Reg()
            dma("pool", wmo_sb[:], G["wmo_c"].ap().rearrange("(a p) n -> p a n", p=128), writes=[r_wmo])
            pring = Ring([pq[:, i, :] for i in range(8)])
            sring = Ring([stg[:, i, :] for i in range(3)])
            partv = G["part"].ap()
            k = 0
            for blk in range(S // 128):
                st, r_st = sring.next()
                for cc in range(4):
                    ps, r_ps = pring.next()
                    op("pe", lambda e, ps=ps, cc=cc: e.matmul(
                        ps, lhsT=mixA[:, blk * 128:(blk + 1) * 128], rhs=wmo_sb[:, 0, cc * 512:(cc + 1) * 512],
                        start=True, stop=False), reads=[r_mA[blk // 4], r_wmo], writes=[r_ps])
                    op("pe", lambda e, ps=ps, cc=cc: e.matmul(
                        ps, lhsT=mixB[:, blk * 128:(blk + 1) * 128], rhs=wmo_sb[:, 1, cc * 512:(cc + 1) * 512],
                        start=False, stop=True), reads=[r_mB[blk], r_wmo], writes=[r_ps])
                    if k % 2 == 0:
                        op("act", lambda e, ps=ps, cc=cc, st=st: e.activation(
                            out=st[:, cc * 512:(cc + 1) * 512], in_=ps, func=AF.Copy),
                           reads=[r_ps], writes=[r_st])
                    else:
                        op("dve", lambda e, ps=ps, cc=cc, st=st: e.tensor_copy(
                            out=st[:, cc * 512:(cc + 1) * 512], in_=ps),
                           reads=[r_ps], writes=[r_st])
                    k += 1
                dma("sp", partv[blk * 128:(blk + 1) * 128, :], st, reads=[r_st], writes=[G["r_part"][blk]])
        SC.barrier()


def layer_norm(nc, SC, h, r_h, lnp, r_lnp, gi, tmp):
    op = SC.op
    st6, st2, rstd, neghalf = tmp["st6"], tmp["st2"], tmp["rstd"], tmp["neghalf"]
    for tc in range(NTC):
        k = tc % 2
        r6, r2, rr = tmp["r6"][k], tmp["r2"][k], tmp["rr"][k]
        for q in range(4):
            op("dve", lambda e, q=q: e.bn_stats(out=st6[:, k, q * 6:(q + 1) * 6], in_=h[:, tc, q * 512:(q + 1) * 512]),
               reads=[r_h[tc]], writes=[r6])
        op("dve", lambda e: e.bn_aggr(out=st2[:, k, :], in_=st6[:, k, :]), reads=[r6], writes=[r2])
        op("dve", lambda e: e.tensor_scalar(out=rstd[:, k, :], in0=st2[:, k, 1:2], scalar1=LN_EPS,
                                            scalar2=None, op0=ALU.add), reads=[r2], writes=[rr])
        op("pool", lambda e: e.tensor_tensor(out=rstd[:, k, :], in0=rstd[:, k, :], in1=neghalf[:],
                                             op=ALU.pow), reads=[rr, tmp["r_nh"]], writes=[rr])
        op("dve", lambda e: e.tensor_scalar(out=h[:, tc, :], in0=h[:, tc, :], scalar1=st2[:, k, 0:1],
                                            scalar2=rstd[:, k, :], op0=ALU.subtract, op1=ALU.mult),
           reads=[r_h[tc], r2, rr], writes=[r_h[tc]])
        op("pool", lambda e: e.tensor_tensor(out=h[:, tc, :], in0=h[:, tc, :], in1=lnp[:, 0, :], op=ALU.mult),
           reads=[r_h[tc], r_lnp], writes=[r_h[tc]])
        op("dve", lambda e: e.tensor_tensor(out=h[:, tc, :], in0=h[:, tc, :], in1=lnp[:, 1, :], op=ALU.add),
           reads=[r_h[tc], r_lnp], writes=[r_h[tc]])


def phase23(nc, SC, G, debug):
    op, dma = SC.op, SC.dma
    R = G["R"]
    ident, r_ident, ones_f, r_ones = G["ident"], G["r_ident"], G["ones_f"], G["r_ones"]
    lnv = G["ln_par"].ap().partition_broadcast(128)[:, 0, :]
    with ExitStack() as es0:
        h = es0.enter_context(nc.sbuf_tensor("h", [128, NTC, D], F32))
        wr = [es0.enter_context(nc.sbuf_tensor("wr%d" % i, [128, 8192], BF16)) for i in range(2)]
        st6 = es0.enter_context(nc.sbuf_tensor("ln_st6", [128, 2, 24], F32))
        st2 = es0.enter_context(nc.sbuf_tensor("ln_st2", [128, 2, 2], F32))
        rstd = es0.enter_context(nc.sbuf_tensor("ln_rstd", [128, 2, 1], F32))
        neghalf = es0.enter_context(nc.sbuf_tensor("ln_nh", [128, 1], F32))
        ps = es0.enter_context(nc.psum_tensor("ps", [128, 8, 512], F32))
        lt = dict(st6=st6, st2=st2, rstd=rstd, neghalf=neghalf, r6=[Reg(), Reg()], r2=[Reg(), Reg()],
                  rr=[Reg(), Reg()], r_nh=Reg())
        op("dve", lambda e: e.memset(neghalf[:], -0.5), writes=[lt["r_nh"]])
        r_h = [Reg() for _ in range(NTC)]
        wring = Ring(wr)
        bank = [(ps[:, i, :], Reg()) for i in range(8)]
        bi = [0]

        def nbank():
            b = bank[bi[0] % 8]
            bi[0] += 1
            return b

        def w16(buf):
            return buf[:].rearrange("p (a b) -> p a b", a=16)

        def load_w(dram, cc, r_src):
            buf, r_buf = wring.next()
            dma("sp", w16(buf), dram.ap().rearrange("(dc p) n -> p dc n", p=128)[:, :, cc * 512:(cc + 1) * 512],
                reads=[r_src], writes=[r_buf])
            return w16(buf), r_buf

        def do_ln(i):
            with ExitStack() as esl:
                lnp = esl.enter_context(nc.sbuf_tensor("lnp%d" % i, [128, 2, D], F32))
                r_lnp = Reg()
                dma("sp", lnp[:, 0, :], lnv[:, (2 * i) * D:(2 * i + 1) * D], writes=[r_lnp])
                dma("sp", lnp[:, 1, :], lnv[:, (2 * i + 1) * D:(2 * i + 2) * D], writes=[r_lnp])
                layer_norm(nc, SC, h, r_h, lnp, r_lnp, i, lt)
                SC.barrier()

        dma("sp", h[:], G["x_tok"].ap().rearrange("(tc p) d -> p tc d", p=128), writes=r_h)
        with ExitStack() as es:
            mxb = es.enter_context(nc.sbuf_tensor("mxb", [128, 2, D], F32))
            r_mxb = [Reg(), Reg()]
            mixv = G["mixed"].ap().rearrange("(tc p) d -> p tc d", p=128)
            for tc in range(NTC):
                k = tc % 2
                dma("sp", mxb[:, k, :], mixv[:, tc, :], reads=[R["mixed"]], writes=[r_mxb[k]])
                op("dve", lambda e, tc=tc, k=k: e.scalar_tensor_tensor(
                    out=h[:, tc, :], in0=h[:, tc, :], scalar=DN_ALPHA, in1=mxb[:, k, :],
                    op0=ALU.mult, op1=ALU.add), reads=[r_h[tc], r_mxb[k]], writes=[r_h[tc]])
            SC.barrier()
        do_ln(0)

        with ExitStack() as es:
            kT = es.enter_context(nc.sbuf_tensor("kT", [128, 16, MEM], BF16))
            v_sb = es.enter_context(nc.sbuf_tensor("v_sb", [128, 2, D], BF16))
            qT = es.enter_context(nc.sbuf_tensor("qT", [128, 16, TPC], BF16))
            r_kT, r_v, r_qT = Reg(), Reg(), Reg()
            with ExitStack() as es2:
                memT_sb = es2.enter_context(nc.sbuf_tensor("memT_sb", [128, 16, MEM], BF16))
                r_memT = Reg()
                dma("pool", memT_sb[:], G["memT"].ap().rearrange("(dc p) m -> p dc m", p=128), writes=[r_memT])
                for cc in range(4):
                    wb, r_wb = load_w(G["wk_f"], cc, R["wk"])
                    for g in range(4):
                        pb, r_pb = nbank()
                        for dc in range(16):
                            op("pe", lambda e, dc=dc, g=g, pb=pb, wb=wb: e.matmul(
                                pb[:, 0:MEM], lhsT=wb[:, dc, g * 128:(g + 1) * 128], rhs=memT_sb[:, dc, :],
                                start=(dc == 0), stop=(dc == 15)), reads=[r_wb, r_memT], writes=[r_pb])
                        op("act", lambda e, g=g, pb=pb, cc=cc: e.activation(
                            out=kT[:, cc * 4 + g, :], in_=pb[:, 0:MEM], func=AF.Copy), reads=[r_pb], writes=[r_kT])
                for cc in range(4):
                    wb, r_wb = load_w(G["wv_f"], cc, R["wv"])
                    for mc in range(2):
                        pb, r_pb = nbank()
                        for dc in range(16):
                            op("pe", lambda e, dc=dc, mc=mc, pb=pb, wb=wb: e.matmul(
                                pb, lhsT=memT_sb[:, dc, mc * 128:(mc + 1) * 128], rhs=wb[:, dc, :],
                                start=(dc == 0), stop=(dc == 15)), reads=[r_wb, r_memT], writes=[r_pb])
                        op("dve", lambda e, mc=mc, pb=pb, cc=cc: e.tensor_copy(
                            out=v_sb[:, mc, cc * 512:(cc + 1) * 512], in_=pb), reads=[r_pb], writes=[r_v])
                SC.barrier()
            with ExitStack() as es2:
                hb = es2.enter_context(nc.sbuf_tensor("hb", [128, 2, D], BF16))
                hT = es2.enter_context(nc.sbuf_tensor("hT", [128, 16, TPC], BF16))
                r_hb, r_hT = [Reg(), Reg()], Reg()
                for tc in range(NTC):
                    k = tc % 2
                    op("act", lambda e, tc=tc, k=k: e.activation(out=hb[:, k, :], in_=h[:, tc, :], func=AF.Copy),
                       reads=[r_h[tc]], writes=[r_hb[k]])
                    for half in range(2):
                        pb, r_pb = nbank()
                        pbb = pb.bitcast(BF16)
                        for j in range(8):
                            dc = half * 8 + j
                            op("pe", lambda e, j=j, dc=dc, k=k, pbb=pbb: e.transpose(
                                out=pbb[:, j * 128:(j + 1) * 128], in_=hb[:, k, dc * 128:(dc + 1) * 128],
                                identity=ident[:]), reads=[r_hb[k], r_ident], writes=[r_pb])
                        eng = "act" if half == 0 else "dve"
                        src = pbb.rearrange("p (a b) -> p a b", a=8)
                        dst = hT[:, half * 8:(half + 1) * 8, tc * 128:(tc + 1) * 128]
                        if eng == "act":
                            op("act", lambda e, src=src, dst=dst: e.activation(out=dst, in_=src, func=AF.Copy),
                               reads=[r_pb], writes=[r_hT])
                        else:
                            op("dve", lambda e, src=src, dst=dst: e.tensor_copy(out=dst, in_=src),
                               reads=[r_pb], writes=[r_hT])
                for cc in range(4):
                    wb, r_wb = load_w(G["wq_f"], cc, R["wq"])
                    for g in range(4):
                        for th in range(2):
                            pb, r_pb = nbank()
                            for dc in range(16):
                                op("pe", lambda e, dc=dc, g=g, th=th, pb=pb, wb=wb: e.matmul(
                                    pb, lhsT=wb[:, dc, g * 128:(g + 1) * 128],
                                    rhs=hT[:, dc, th * 512:(th + 1) * 512],
                                    start=(dc == 0), stop=(dc == 15)), reads=[r_wb, r_hT], writes=[r_pb])
                            op("act", lambda e, g=g, th=th, pb=pb, cc=cc: e.activation(
                                out=qT[:, cc * 4 + g, th * 512:(th + 1) * 512], in_=pb, func=AF.Copy,
                                scale=512.0 ** -0.5), reads=[r_pb], writes=[r_qT])
                SC.barrier()
            with ExitStack() as es2:
                oT = es2.enter_context(nc.sbuf_tensor("oT", [128, 16, TPC], BF16))
                p_f = es2.enter_context(nc.sbuf_tensor("p_f", [128, 2, 4 * MEM], F32))
                p_b = es2.enter_context(nc.sbuf_tensor("p_b", [128, 2, 4 * MEM], BF16))
                pT = es2.enter_context(nc.sbuf_tensor("pT", [128, 2, 8, 128], BF16))
                sm = es2.enter_context(nc.sbuf_tensor("sm", [128, 2, 16], F32))
                r_oT, r_pf, r_pb2, r_pT, r_sm = Reg(), [Reg(), Reg()], [Reg(), Reg()], [Reg(), Reg()], [Reg(), Reg()]
                for tc in range(NTC):
                    k = tc % 2
                    tok = slice(tc * 128, (tc + 1) * 128)
                    sc2 = []
                    for pair in range(2):
                        pb, r_pb = nbank()
                        sc2.append((pb, r_pb))
                        for hh in range(2):
                            hd = pair * 2 + hh
                            for j in range(4):
                                op("pe", lambda e, hd=hd, hh=hh, j=j, pb=pb: e.matmul(
                                    pb[:, hh * MEM:(hh + 1) * MEM], lhsT=qT[:, hd * 4 + j, tok],
                                    rhs=kT[:, hd * 4 + j, :], start=(j == 0), stop=(j == 3)),
                                   reads=[r_qT, r_kT], writes=[r_pb])
                    for pair in range(2):
                        pb, r_pb = sc2[pair]
                        op("dve", lambda e, pb=pb, pair=pair: e.tensor_reduce(
                            out=sm[:, k, pair * 2:pair * 2 + 2], in_=pb.rearrange("p (a b) -> p a b", a=2),
                            axis=AX.X, op=ALU.max), reads=[r_pb], writes=[r_sm[k]])
                    op("dve", lambda e: e.tensor_scalar(out=sm[:, k, 4:8], in0=sm[:, k, 0:4], scalar1=-1.0,
                                                        scalar2=None, op0=ALU.mult),
                       reads=[r_sm[k]], writes=[r_sm[k]])
                    for hd in range(4):
                        pb, r_pb = sc2[hd // 2]
                        hh = hd % 2
                        op("act", lambda e, hd=hd, hh=hh, pb=pb: e.activation(
                            out=p_f[:, k, hd * MEM:(hd + 1) * MEM], in_=pb[:, hh * MEM:(hh + 1) * MEM],
                            func=AF.Exp, bias=sm[:, k, 4 + hd:5 + hd], accum_out=sm[:, k, 8 + hd:9 + hd]),
                           reads=[r_pb, r_sm[k]], writes=[r_pf[k], r_sm[k]])
                    op("dve", lambda e: e.reciprocal(out=sm[:, k, 12:16], in_=sm[:, k, 8:12]),
                       reads=[r_sm[k]], writes=[r_sm[k]])
                    for hd in range(4):
                        op("dve", lambda e, hd=hd: e.tensor_scalar(
                            out=p_b[:, k, hd * MEM:(hd + 1) * MEM], in0=p_f[:, k, hd * MEM:(hd + 1) * MEM],
                            scalar1=sm[:, k, 12 + hd:13 + hd], scalar2=None, op0=ALU.mult),
                           reads=[r_pf[k], r_sm[k]], writes=[r_pb2[k]])
                    pb, r_pb = nbank()
                    pbb = pb.bitcast(BF16)
                    for j in range(8):
                        op("pe", lambda e, j=j, pbb=pbb: e.transpose(
                            out=pbb[:, j * 128:(j + 1) * 128], in_=p_b[:, k, j * 128:(j + 1) * 128],
                            identity=ident[:]), reads=[r_pb2[k], r_ident], writes=[r_pb])
                    op("act", lambda e, pbb=pbb: e.activation(
                        out=pT[:, k, :, :], in_=pbb.rearrange("p (a b) -> p a b", a=8), func=AF.Copy),
                       reads=[r_pb], writes=[r_pT[k]])
                    for q4 in range(4):
                        pb, r_pb = nbank()
                        for jj in range(4):
                            fc = q4 * 4 + jj
                            hd = fc // 4
                            for mc in range(2):
                                op("pe", lambda e, jj=jj, fc=fc, hd=hd, mc=mc, pb=pb: e.matmul(
                                    pb[:, jj * 128:(jj + 1) * 128], lhsT=v_sb[:, mc, fc * 128:(fc + 1) * 128],
                                    rhs=pT[:, k, hd * 2 + mc, :], start=(mc == 0), stop=(mc == 1)),
                                   reads=[r_v, r_pT[k]], writes=[r_pb])
                        src = pb.rearrange("p (a b) -> p a b", a=4)
                        dst = oT[:, q4 * 4:(q4 + 1) * 4, tok]
                        if q4 % 2 == 0:
                            op("dve", lambda e, src=src, dst=dst: e.tensor_copy(out=dst, in_=src),
                               reads=[r_pb], writes=[r_oT])
                        else:
                            op("act", lambda e, src=src, dst=dst: e.activation(out=dst, in_=src, func=AF.Copy),
                               reads=[r_pb], writes=[r_oT])
                for cc in range(4):
                    wb, r_wb = load_w(G["wo_f"], cc, R["wo"])
                    for tc in range(NTC):
                        pb, r_pb = nbank()
                        for fc in range(16):
                            op("pe", lambda e, fc=fc, tc=tc, pb=pb, wb=wb: e.matmul(
                                pb, lhsT=oT[:, fc, tc * 128:(tc + 1) * 128], rhs=wb[:, fc, :],
                                start=(fc == 0), stop=(fc == 15)), reads=[r_wb, r_oT], writes=[r_pb])
                        op("dve", lambda e, tc=tc, cc=cc, pb=pb: e.scalar_tensor_tensor(
                            out=h[:, tc, cc * 512:(cc + 1) * 512], in0=h[:, tc, cc * 512:(cc + 1) * 512],
                            scalar=DN_ALPHA, in1=pb, op0=ALU.mult, op1=ALU.add),
                           reads=[r_h[tc], r_pb], writes=[r_h[tc]])
                SC.barrier()
            do_ln(1)

        if debug == "p2":
            dma("sp", G["out"].ap().rearrange("(tc p) d -> p tc d", p=128), h[:], reads=r_h, writes=[R["out"]])
            return
        phase3(nc, SC, G, h, r_h, wr, nbank)
        do_ln(2)
        dma("sp", G["out"].ap().rearrange("(tc p) d -> p tc d", p=128), h[:], reads=r_h, writes=[R["out"]])
        SC.wait_all("sp", [R["out"]])


def phase3(nc, SC, G, h, r_h, wr2, nbank):
    op, dma = SC.op, SC.dma
    R = G["R"]
    ident, r_ident, ones_f, r_ones = G["ident"], G["r_ident"], G["ones_f"], G["r_ones"]
    I32 = mybir.dt.int32
    with ExitStack() as es:
        def sb(name, shape, dt):
            return es.enter_context(nc.sbuf_tensor(name, shape, dt))
        esA = ExitStack()

        def sa(name, shape, dt):
            return esA.enter_context(nc.sbuf_tensor(name, shape, dt))
        wr3 = sb("wr3", [128, 8192], BF16)
        wring = Ring(list(wr2) + [wr3])
        hb_all = sb("hb_all", [128, NTC, D], BF16)
        gsel = sb("gsel", [128, NTC, 8], F32)
        gsel_b = sb("gsel_b", [128, NTC, 8], BF16)
        gate = sb("gate", [128, NTC, 8], F32)
        gate_hi = sb("gate_hi", [128, NTC, 8], BF16)
        gate_lo = sb("gate_lo", [128, NTC, 8], BF16)
        slot = sb("slot", [128, NTC], F32)
        identf = sa("identf", [128, 128], F32)
        ustr = sa("ustr", [128, 128], BF16)
        ones_b = sa("ones_b", [128, 128], BF16)
        wrt = sa("wrt", [128, 16, 72], F32)
        brt = sa("brt", [128, 72], F32)
        hTf = sa("hTf", [128, 16, 128], F32)
        rt = sa("rt", [128, 160], F32)
        r_c = Reg()
        r_hb = [Reg() for _ in range(NTC)]
        r_hTf, r_rt, r_rout = Reg(), Reg(), Reg()
        r_P, r_PT, r_X, r_gs = Reg(), Reg(), Reg(), Reg()
        r_yacc, r_ybf = [Reg() for _ in range(NSB)], Reg()
        r_s, r_hid, r_hidT = [Reg(), Reg()], [Reg(), Reg()], [Reg(), Reg()]

        op("pool", lambda e: e.affine_select(out=identf[:], in_=ones_f[:, 0:128], pattern=[[1, 128]],
                                             compare_op=ALU.is_equal, fill=0.0, base=0, channel_multiplier=-1),
           reads=[r_ones], writes=[r_c])
        op("pool", lambda e: e.affine_select(out=ustr[:], in_=ones_f[:, 0:128], pattern=[[1, 128]],
                                             compare_op=ALU.is_gt, fill=0.0, base=0, channel_multiplier=-1),
           reads=[r_ones], writes=[r_c])
        op("dve", lambda e: e.tensor_copy(out=ones_b[:], in_=ones_f[:, 0:128]), reads=[r_ones], writes=[r_c])
        dma("sp", wrt[:], G["w_rt"].ap().rearrange("(dc p) n -> p dc n", p=128), writes=[r_c])
        dma("sp", brt[:], G["b_rt"].ap().partition_broadcast(128)[:, 0, :], writes=[r_c])

        for tc in range(NTC):
            op("act", lambda e, tc=tc: e.activation(out=hb_all[:, tc, :], in_=h[:, tc, :], func=AF.Copy),
               reads=[r_h[tc]], writes=[r_hb[tc]])
            for q4 in range(4):
                pb, r_pb = nbank()
                for j in range(4):
                    dc = q4 * 4 + j
                    op("pe", lambda e, j=j, dc=dc, pb=pb, tc=tc: e.transpose(
                        out=pb[:, j * 128:(j + 1) * 128], in_=h[:, tc, dc * 128:(dc + 1) * 128],
                        identity=identf[:]), reads=[r_h[tc], r_c], writes=[r_pb])
                src = pb.rearrange("p (a b) -> p a b", a=4)
                dst = hTf[:, q4 * 4:(q4 + 1) * 4, :]
                if q4 % 2 == 0:
                    op("dve", lambda e, src=src, dst=dst: e.tensor_copy(out=dst, in_=src), reads=[r_pb], writes=[r_hTf])
                else:
                    op("act", lambda e, src=src, dst=dst: e.activation(out=dst, in_=src, func=AF.Copy),
                       reads=[r_pb], writes=[r_hTf])
            pl, r_pl = nbank()
            for dc in range(16):
                op("pe", lambda e, dc=dc, pl=pl: e.matmul(pl[:, 0:72], lhsT=hTf[:, dc, :], rhs=wrt[:, dc, :],
                                                          start=(dc == 0), stop=(dc == 15)),
                   reads=[r_hTf, r_c], writes=[r_pl])
            op("pool", lambda e, tc=tc: e.tensor_scalar(out=h[:, tc, :], in0=h[:, tc, :], scalar1=DN_ALPHA,
                                                        scalar2=None, op0=ALU.mult),
               reads=[r_h[tc]], writes=[r_h[tc]])
            lg = rt[:, 0:72]
            V = lambda a, b: rt[:, a:b]
            gmax, ngmax, gsum, gval = V(72, 73), V(73, 74), V(74, 75), V(75, 76)
            gexp, el, oh1, el2, oh2 = V(76, 84), V(84, 92), V(92, 100), V(100, 108), V(108, 116)
            m1, m2, dd, ed, den, rden, ga, gb = (V(116 + i, 117 + i) for i in range(8))
            tmp8 = V(124, 132)
            D1 = lambda fn, extra_r=(), extra_w=(): op("dve", fn, reads=[r_rt] + list(extra_r),
                                                       writes=[r_rt] + list(extra_w))
            D1(lambda e: e.tensor_tensor(out=lg, in0=pl[:, 0:72], in1=brt[:], op=ALU.add), extra_r=[r_pl, r_c])
            D1(lambda e: e.reduce_max(out=gmax, in_=rt[:, 0:8], axis=AX.X))
            D1(lambda e, tc=tc: e.tensor_scalar(out=gsel[:, tc, :], in0=rt[:, 0:8], scalar1=gmax, scalar2=None,
                                                op0=ALU.is_equal), extra_w=[r_rout])
            D1(lambda e, tc=tc: e.tensor_copy(out=gsel_b[:, tc, :], in_=gsel[:, tc, :]), extra_r=[r_rout],
               extra_w=[r_rout])
            D1(lambda e: e.tensor_scalar(out=ngmax, in0=gmax, scalar1=-1.0, scalar2=None, op0=ALU.mult))
            op("act", lambda e: e.activation(out=gexp, in_=rt[:, 0:8], func=AF.Exp, bias=ngmax, accum_out=gsum),
               reads=[r_rt], writes=[r_rt])
            D1(lambda e: e.reciprocal(out=gval, in_=gsum))
            for g in range(8):
                if g == 0:
                    D1(lambda e, tc=tc: e.tensor_scalar(out=el, in0=rt[:, 8:16], scalar1=gsel[:, tc, 0:1],
                                                        scalar2=None, op0=ALU.mult), extra_r=[r_rout])
                else:
                    D1(lambda e, tc=tc, g=g: e.scalar_tensor_tensor(
                        out=el, in0=rt[:, 8 + g * 8:16 + g * 8], scalar=gsel[:, tc, g:g + 1], in1=el,
                        op0=ALU.mult, op1=ALU.add), extra_r=[r_rout])
            D1(lambda e: e.reduce_max(out=m1, in_=el, axis=AX.X))
            D1(lambda e: e.tensor_scalar(out=oh1, in0=el, scalar1=m1, scalar2=None, op0=ALU.is_equal))
            D1(lambda e: e.scalar_tensor_tensor(out=el2, in0=oh1, scalar=-1e30, in1=el, op0=ALU.mult, op1=ALU.add))
            D1(lambda e: e.reduce_max(out=m2, in_=el2, axis=AX.X))
            D1(lambda e: e.tensor_scalar(out=oh2, in0=el2, scalar1=m2, scalar2=None, op0=ALU.is_equal))
            D1(lambda e: e.tensor_tensor(out=dd, in0=m2, in1=m1, op=ALU.subtract))
            op("act", lambda e: e.activation(out=ed, in_=dd, func=AF.Exp), reads=[r_rt], writes=[r_rt])
            D1(lambda e: e.tensor_scalar(out=den, in0=ed, scalar1=1.0, scalar2=None, op0=ALU.add))
            D1(lambda e: e.reciprocal(out=rden, in_=den))
            D1(lambda e: e.tensor_tensor(out=ga, in0=rden, in1=gval, op=ALU.mult))
            D1(lambda e: e.tensor_tensor(out=gb, in0=ga, in1=ed, op=ALU.mult))
            D1(lambda e: e.tensor_scalar(out=tmp8, in0=oh1, scalar1=ga, scalar2=None, op0=ALU.mult))
            D1(lambda e, tc=tc: e.scalar_tensor_tensor(out=gate[:, tc, :], in0=oh2, scalar=gb, in1=tmp8,
                                                       op0=ALU.mult, op1=ALU.add), extra_w=[r_rout])
            D1(lambda e, tc=tc: e.tensor_copy(out=gate_hi[:, tc, :], in_=gate[:, tc, :]), extra_r=[r_rout],
               extra_w=[r_rout])
            D1(lambda e, tc=tc: e.tensor_tensor(out=gate_lo[:, tc, :], in0=gate[:, tc, :], in1=gate_hi[:, tc, :],
                                                op=ALU.subtract), extra_r=[r_rout], extra_w=[r_rout])
            pr, r_pr = nbank()
            op("pe", lambda e, pr=pr, tc=tc: e.matmul(pr[:, 0:8], lhsT=ustr[:], rhs=gsel_b[:, tc, :],
                                                      start=True, stop=(tc == 0)),
               reads=[r_c, r_rout], writes=[r_pr])
            for t2 in range(tc):
                op("pe", lambda e, pr=pr, t2=t2, tc=tc: e.matmul(pr[:, 0:8], lhsT=ones_b[:], rhs=gsel_b[:, t2, :],
                                                                 start=False, stop=(t2 == tc - 1)),
                   reads=[r_c, r_rout], writes=[r_pr])
            D1(lambda e, pr=pr, tc=tc: e.tensor_tensor(out=tmp8, in0=pr[:, 0:8], in1=gsel[:, tc, :], op=ALU.mult),
               extra_r=[r_pr, r_rout])
            D1(lambda e, tc=tc: e.reduce_sum(out=slot[:, tc:tc + 1], in_=tmp8, axis=AX.X), extra_w=[r_rout])

        SC.barrier()
        esA.close()
        iota_f = sb("iota_f", [128, CAP], F32)
        P = sb("P", [128, NTC, CAP], BF16)
        PT = sb("PT", [128, NSB, TPC], BF16)
        XgT = sb("XgT", [128, 16, CAP], BF16)
        gs = sb("gs", [128, NSB, 8], F32)
        y_acc = sb("y_acc", [128, NSB, D], F32)
        y_bf = sb("y_bf", [128, NSB, D], BF16)
        s_sb = sb("s_sb", [128, 2, 512], F32)
        hid = sb("hid", [128, 2, 512], BF16)
        hidT = sb("hidT", [128, 2, 4, 128], BF16)
        r_c2 = Reg()
        op("dve", lambda e: e.tensor_tensor_scan(out=iota_f[:], data0=ones_f[:, 0:CAP], data1=ones_f[:, 0:CAP],
                                                 initial=-1.0, op0=ALU.mult, op1=ALU.add),
           reads=[r_ones], writes=[r_c2])

        def w16(buf):
            return buf[:].rearrange("p (a b) -> p a b", a=16)

        def w4(buf):
            return buf[:].rearrange("p (a b) -> p a b", a=4)

        w1v, w3v, w2v = G["w1_f"].ap(), G["w3_f"].ap(), G["w2_f"].ap()
        kk = 0
        for g in range(8):
            for tc in range(NTC):
                op("dve", lambda e, tc=tc, g=g: e.tensor_scalar(
                    out=P[:, tc, :], in0=iota_f[:], scalar1=slot[:, tc:tc + 1], scalar2=gsel[:, tc, g:g + 1],
                    op0=ALU.is_equal, op1=ALU.mult), reads=[r_c2, r_rout], writes=[r_P])
            for dc in range(16):
                pb, r_pb = nbank()
                for tc in range(NTC):
                    op("pe", lambda e, dc=dc, tc=tc, pb=pb: e.matmul(
                        pb[:, 0:CAP], lhsT=hb_all[:, tc, dc * 128:(dc + 1) * 128], rhs=P[:, tc, :],
                        start=(tc == 0), stop=(tc == NTC - 1)), reads=[r_hb[tc], r_P], writes=[r_pb])
                if dc % 2 == 0:
                    op("act", lambda e, dc=dc, pb=pb: e.activation(out=XgT[:, dc, :], in_=pb[:, 0:CAP], func=AF.Copy),
                       reads=[r_pb], writes=[r_X])
                else:
                    op("dve", lambda e, dc=dc, pb=pb: e.tensor_copy(out=XgT[:, dc, :], in_=pb[:, 0:CAP]),
                       reads=[r_pb], writes=[r_X])
            for sbk in range(NSB):
                pb, r_pb = nbank()
                n = 0
                for gg in (gate_hi, gate_lo):
                    for tc in range(NTC):
                        op("pe", lambda e, tc=tc, sbk=sbk, gg=gg, pb=pb, n=n: e.matmul(
                            pb[:, 0:8], lhsT=P[:, tc, sbk * 128:(sbk + 1) * 128], rhs=gg[:, tc, :],
                            start=(n == 0), stop=(n == 2 * NTC - 1)), reads=[r_P, r_rout], writes=[r_pb])
                        n += 1
                op("dve", lambda e, sbk=sbk, pb=pb: e.tensor_copy(out=gs[:, sbk, :], in_=pb[:, 0:8]),
                   reads=[r_pb], writes=[r_gs])
                pb, r_pb = nbank()
                pbb = pb.bitcast(BF16)
                for tc in range(NTC):
                    op("pe", lambda e, tc=tc, sbk=sbk, pbb=pbb: e.transpose(
                        out=pbb[:, tc * 128:(tc + 1) * 128], in_=P[:, tc, sbk * 128:(sbk + 1) * 128],
                        identity=ident[:]), reads=[r_P, r_ident], writes=[r_pb])
                op("act", lambda e, sbk=sbk, pbb=pbb: e.activation(out=PT[:, sbk, :], in_=pbb, func=AF.Copy),
                   reads=[r_pb], writes=[r_PT])
            for ee in range(8):
                E = g * 8 + ee
                b1, r_b1 = wring.next()
                dma("sp", w16(b1), w1v[E * D:(E + 1) * D, :].rearrange("(dc p) n -> p dc n", p=128),
                    reads=[R["w1"]], writes=[r_b1])
                b3, r_b3 = wring.next()
                dma("sp", w16(b3), w3v[E * D:(E + 1) * D, :].rearrange("(dc p) n -> p dc n", p=128),
                    reads=[R["w3"]], writes=[r_b3])
                b2, r_b2 = wring.next()
                dma("sp", w4(b2), w2v[E * 512:(E + 1) * 512, :].rearrange("(kc p) n -> p kc n", p=128),
                    reads=[R["w2"]], writes=[r_b2])
                W1, W3, W2 = w16(b1), w16(b3), w4(b2)
                for sbk in range(NSB):
                    k = kk % 2
                    kk += 1
                    sl = slice(sbk * 128, (sbk + 1) * 128)
                    p1, r_p1 = nbank()
                    for dc in range(16):
                        op("pe", lambda e, dc=dc, p1=p1: e.matmul(p1, lhsT=XgT[:, dc, sl], rhs=W1[:, dc, :],
                                                                  start=(dc == 0), stop=(dc == 15)),
                           reads=[r_X, r_b1], writes=[r_p1])
                    p3, r_p3 = nbank()
                    for dc in range(16):
                        op("pe", lambda e, dc=dc, p3=p3: e.matmul(p3, lhsT=XgT[:, dc, sl], rhs=W3[:, dc, :],
                                                                  start=(dc == 0), stop=(dc == 15)),
                           reads=[r_X, r_b3], writes=[r_p3])
                    op("act", lambda e, p1=p1: e.activation(out=s_sb[:, k, :], in_=p1, func=AF.Silu),
                       reads=[r_p1], writes=[r_s[k]])
                    op("dve", lambda e, p3=p3: e.scalar_tensor_tensor(
                        out=hid[:, k, :], in0=p3, scalar=gs[:, sbk, ee:ee + 1], in1=s_sb[:, k, :],
                        op0=ALU.mult, op1=ALU.mult), reads=[r_p3, r_gs, r_s[k]], writes=[r_hid[k]])
                    pt, r_pt = nbank()
                    ptb = pt.bitcast(BF16)
                    for kc in range(4):
                        op("pe", lambda e, kc=kc, ptb=ptb: e.transpose(
                            out=ptb[:, kc * 128:(kc + 1) * 128], in_=hid[:, k, kc * 128:(kc + 1) * 128],
                            identity=ident[:]), reads=[r_hid[k], r_ident], writes=[r_pt])
                    op("act", lambda e, ptb=ptb: e.activation(
                        out=hidT[:, k, :, :], in_=ptb[:, 0:512].rearrange("p (a b) -> p a b", a=4), func=AF.Copy),
                       reads=[r_pt], writes=[r_hidT[k]])
                    for cc in range(4):
                        py, r_py = nbank()
                        for kc in range(4):
                            op("pe", lambda e, kc=kc, cc=cc, py=py: e.matmul(
                                py, lhsT=hidT[:, k, kc, :], rhs=W2[:, kc, cc * 512:(cc + 1) * 512],
                                start=(kc == 0), stop=(kc == 3)), reads=[r_hidT[k], r_b2], writes=[r_py])
                        ya = y_acc[:, sbk, cc * 512:(cc + 1) * 512]
                        if ee == 0:
                            op("dve", lambda e, ya=ya, py=py: e.tensor_copy(out=ya, in_=py),
                               reads=[r_py], writes=[r_yacc[sbk]])
                        else:
                            op("dve", lambda e, ya=ya, py=py: e.tensor_tensor(out=ya, in0=ya, in1=py, op=ALU.add),
                               reads=[r_py, r_yacc[sbk]], writes=[r_yacc[sbk]])
            for sbk in range(NSB):
                op("act", lambda e, sbk=sbk: e.activation(out=y_bf[:, sbk, :], in_=y_acc[:, sbk, :], func=AF.Copy),
                   reads=[r_yacc[sbk]], writes=[r_ybf])
            for tc in range(NTC):
                for cc in range(4):
                    pb, r_pb = nbank()
                    for sbk in range(NSB):
                        op("pe", lambda e, sbk=sbk, tc=tc, cc=cc, pb=pb: e.matmul(
                            pb, lhsT=PT[:, sbk, tc * 128:(tc + 1) * 128], rhs=y_bf[:, sbk, cc * 512:(cc + 1) * 512],
                            start=(sbk == 0), stop=(sbk == NSB - 1)), reads=[r_PT, r_ybf], writes=[r_pb])
                    hs = h[:, tc, cc * 512:(cc + 1) * 512]
                    op("dve", lambda e, hs=hs, pb=pb: e.tensor_tensor(out=hs, in0=hs, in1=pb, op=ALU.add),
                       reads=[r_pb, r_h[tc]], writes=[r_h[tc]])
        SC.barrier()


def make_in_maps(inp):
    f = lambda a: np.ascontiguousarray(np.asarray(a, dtype=np.float32))
    x = f(inp["x"])[0]
    xr = x[::-1]
    xT = f(xr.T)
    w_in = f(inp["w_in"])[0]
    gm_ln_g, gm_ln_b = f(inp["gm_ln_g"])[0], f(inp["gm_ln_b"])[0]
    gm_w_s, gm_b_s = f(inp["gm_w_s"])[0], f(inp["gm_b_s"])[0]
    wmo = f(inp["w_mix_out"])[0]
    wq, wk, wv, wo = (f(inp[k])[0] for k in ("mem_w_q", "mem_w_k", "mem_w_v", "mem_w_o"))
    memT = f(f(inp["mem"])[0].T)
    ln_par = f(np.concatenate([f(inp[k])[0] for k in
                               ("ln1_g", "ln1_b", "ln2_g", "ln2_b", "ln3_g", "ln3_b")])[None, :])
    w_rt = f(np.concatenate([f(inp["w_group"])[0],
                             f(inp["w_router"])[0].transpose(1, 0, 2).reshape(D, 64)], axis=1))
    b_rt = f(np.concatenate([f(inp["b_group"])[0], f(inp["b_router"])[0].reshape(64)])[None, :])
    w1, w3, w2 = f(inp["w1"])[0], f(inp["w3"])[0], f(inp["w2"])[0]
    maps = []
    R = D // NCORES
    for c in range(NCORES):
        cols = np.concatenate([2048 + c * 128 + np.arange(128),
                               3072 + c * 128 + np.arange(128),
                               c * 128 + np.arange(128),
                               4096 + c * 128 + np.arange(128),
                               1024 + c * 128 + np.arange(128)])
        m = {
            "xT_sh": f(xT[c * R:(c + 1) * R]),
            "w_in_c": f(w_in[:, cols]),
            "gm_par": f(np.concatenate([gm_ln_g[c], gm_ln_b[c], gm_b_s[c][::-1]])[None, :]),
            "gm_wT": f(gm_w_s[c][::-1, ::-1].T),
            "x_tok": f(xr[c * TPC:(c + 1) * TPC]),
            "wmo_c": f(np.concatenate([wmo[c * 128:(c + 1) * 128],
                                       wmo[1024 + c * 128:1024 + (c + 1) * 128]], axis=0)),
            "wq_sh": f(wq[c * R:(c + 1) * R]),
            "wk_sh": f(wk[c * R:(c + 1) * R]),
            "wv_sh": f(wv[c * R:(c + 1) * R]),
            "wo_sh": f(wo[c * R:(c + 1) * R]),
            "memT": memT, "ln_par": ln_par, "w_rt": w_rt, "b_rt": b_rt,
            "w1_sh": f(w1[c * 8:(c + 1) * 8].reshape(8 * D, 512)),
            "w3_sh": f(w3[c * 8:(c + 1) * 8].reshape(8 * D, 512)),
            "w2_sh": f(w2[c * 8:(c + 1) * 8].reshape(8 * 512, D)),
        }
        maps.append(m)
    return maps


def kernel(**inputs):
    nc = build()
    maps = make_in_maps(inputs)
    res = run_bass_kernel_spmd(nc, maps, core_ids=list(range(NCORES)))
    outs = [np.asarray(res.results[c]["out"], dtype=np.float32) for c in range(NCORES)]
    full = np.concatenate(outs, axis=0)[::-1]
    return np.ascontiguousarray(full)[None, :, :]
```

```python
import numpy as np
from contextlib import ExitStack
import ml_dtypes
import concourse.bass as bass
import concourse.mybir as mybir
from concourse.bass_utils import run_bass_kernel_spmd

F32 = mybir.dt.float32
BF16 = mybir.dt.bfloat16
ALU = mybir.AluOpType
AF = mybir.ActivationFunctionType
AX = mybir.AxisListType

NCORES = 8
S = 8192
D = 2048
TPC = S // NCORES
NTC = TPC // 128
MEM = 256
DN_ALPHA = 2.0 ** 0.25
LN_EPS = 1e-5
GELU_C = 1.5957691216057308
CAP = 256
NSB = CAP // 128


class Reg:
    __slots__ = ("w", "r")

    def __init__(self):
        self.w = None
        self.r = {}


class Sched:
    NDMA = 12

    def __init__(self, nc):
        self.nc = nc
        self.eng = {"pe": nc.tensor, "act": nc.scalar, "dve": nc.vector,
                    "pool": nc.gpsimd, "sp": nc.sync}
        self.sems, self.cnt, self.waited = {}, {}, {}
        for e in ("pe", "act", "dve", "pool"):
            self.sems[e] = nc.alloc_semaphore(name="s_" + e)
            self.cnt[e] = 0
        self.dma_i = {"sp": 0, "pool": 0, "act": 0}
        for q in ("sp", "pool", "act"):
            for i in range(self.NDMA):
                k = "d_%s_%d" % (q, i)
                self.sems[k] = nc.alloc_semaphore(name=k)
                self.cnt[k] = 0
        self.cc_n = 0
        self.n_inst = 0
        self.n_wait = 0

    def _wait(self, e, dep):
        key, val = dep
        if self.waited.get((e, key), 0) >= val:
            return
        self.eng[e].wait_ge(self.sems[key], val)
        self.waited[(e, key)] = val
        self.n_wait += 1

    def _deps(self, e, reads, writes):
        deps = {}

        def add(d):
            if d is None:
                return
            k, v = d
            if k == e and e == "pe":
                return
            if deps.get(k, 0) < v:
                deps[k] = v
        for t in reads:
            add(t.w)
        for t in writes:
            add(t.w)
            for k, v in t.r.items():
                if k != e:
                    add((k, v))
        for k, v in deps.items():
            self._wait(e, (k, v))

    def _mark(self, key, val, reads, writes):
        for t in reads:
            if t.r.get(key, 0) < val:
                t.r[key] = val
        for t in writes:
            t.w = (key, val)
            t.r = {}

    def op(self, e, fn, reads=(), writes=()):
        self._deps(e, reads, writes)
        ins = fn(self.eng[e])
        self.cnt[e] += 1
        ins.then_inc(self.sems[e], 1)
        self._mark(e, self.cnt[e], reads, writes)
        self.n_inst += 1
        return ins

    def dma(self, q, out, in_, reads=(), writes=(), **kw):
        i = self.dma_i[q]
        self.dma_i[q] = i + 1
        key = "d_%s_%d" % (q, i % self.NDMA)
        if self.cnt[key] > 0:
            self._wait(q, (key, self.cnt[key]))
        self._deps(q, reads, writes)
        ins = self.eng[q].dma_start(out=out, in_=in_, **kw)
        self.cnt[key] += 16
        ins.then_inc(self.sems[key], 16)
        self._mark(key, self.cnt[key], reads, writes)
        self.n_inst += 1
        return ins

    def collective(self, kind, alu, in_ap, out_ap, reads=(), writes=()):
        self._deps("pool", reads, writes)
        ins = self.nc.gpsimd.collective_compute(
            kind, alu, replica_groups=[list(range(NCORES))],
            ins=[in_ap.opt()], outs=[out_ap.opt()])
        key = "cc%d" % self.cc_n
        self.cc_n += 1
        self.sems[key] = self.nc.alloc_semaphore(name=key)
        ins.then_inc(self.sems[key])
        self._mark(key, 1, reads, writes)
        return ins

    def allgather(self, in_ap, out_ap, reads=(), writes=(), bounce=None):
        if bounce is not None:
            rb = Reg()
            self.dma("sp", bounce, in_ap, reads=reads, writes=[rb])
            in_ap, reads = bounce, [rb]
        self._deps("pool", reads, writes)
        ins = self.nc.gpsimd.collective_compute(
            "AllGather", ALU.bypass, replica_groups=[list(range(NCORES))],
            ins=[in_ap.opt()], outs=[out_ap.opt()])
        key = "cc%d" % self.cc_n
        self.cc_n += 1
        self.sems[key] = self.nc.alloc_semaphore(name=key)
        ins.then_inc(self.sems[key])
        self._mark(key, 1, reads, writes)
        return ins

    def barrier(self):
        for e in ("pe", "act", "dve", "pool", "sp"):
            for k, v in self.cnt.items():
                if v > 0 and k != e:
                    self._wait(e, (k, v))

    def wait_all(self, e, regs):
        for t in regs:
            if t.w is not None:
                self._wait(e, t.w)


class Ring:
    def __init__(self, bufs):
        self.bufs = [(b, Reg()) for b in bufs]
        self.i = 0

    def next(self):
        b = self.bufs[self.i % len(self.bufs)]
        self.i += 1
        return b


def build(debug=None):
    nc = bass.Bass("TRN2", target_bir_lowering=False)
    SC = Sched(nc)
    op, dma = SC.op, SC.dma

    def din(name, shape, dt=F32):
        return nc.dram_tensor(name, shape, dt, kind="ExternalInput")

    G = {}
    xT_sh = din("xT_sh", [D // NCORES, S])
    G["w_in_c"] = din("w_in_c", [D, 640])
    G["gm_par"] = din("gm_par", [1, 3 * 128])
    G["gm_wT"] = din("gm_wT", [128, 128])
    G["wmo_c"] = din("wmo_c", [256, D])
    xT_full = nc.dram_tensor("xT_full", [D, S], BF16)
    G["xT_full"] = xT_full
    G["part"] = nc.dram_tensor("part", [S, D], F32)
    G["mixed"] = nc.dram_tensor("mixed", [TPC, D], F32)
    R = {k: Reg() for k in ("xT", "part", "mixed", "wq", "wk", "wv", "wo", "w1a", "w3a", "w2a",
                            "w1b", "w3b", "w2b", "out")}
    G["R"] = R
    G["r_part"] = [Reg() for _ in range(S // 128)]
    if debug == "p1":
        out = nc.dram_tensor("out", [TPC, D], F32, kind="ExternalOutput")
    else:
        out = nc.dram_tensor("out", [TPC, D], F32, kind="ExternalOutput")
        G["x_tok"] = din("x_tok", [TPC, D])
        wq_sh = din("wq_sh", [D // NCORES, D])
        wk_sh = din("wk_sh", [D // NCORES, D])
        wv_sh = din("wv_sh", [D // NCORES, D])
        wo_sh = din("wo_sh", [D // NCORES, D])
        G["memT"] = din("memT", [D, MEM])
        G["ln_par"] = din("ln_par", [1, 6 * D])
        G["w_rt"] = din("w_rt", [D, 72])
        G["b_rt"] = din("b_rt", [1, 72])
        if debug != "p2":
            w1_sh = din("w1_sh", [8 * D, 512])
            w3_sh = din("w3_sh", [8 * D, 512])
            w2_sh = din("w2_sh", [8 * 512, D])
            for t in "ab":
                G["w1_f" + t] = nc.dram_tensor("w1_f" + t, [32 * D, 512], BF16)
                G["w3_f" + t] = nc.dram_tensor("w3_f" + t, [32 * D, 512], BF16)
                G["w2_f" + t] = nc.dram_tensor("w2_f" + t, [32 * 512, D], BF16)
        for n in ("wq", "wk", "wv", "wo"):
            G[n + "_f"] = nc.dram_tensor(n + "_f", [D, D], BF16)
    G["out"] = out

    ident = nc.alloc_sbuf_tensor("ident", [128, 128], BF16)
    ones_f = nc.alloc_sbuf_tensor("ones_f", [128, 512], F32)
    r_ident, r_ones = Reg(), Reg()
    op("dve", lambda e: e.memset(ones_f[:], 1.0), writes=[r_ones])
    op("pool", lambda e: e.affine_select(out=ident[:], in_=ones_f[:, 0:128], pattern=[[1, 128]],
                                         compare_op=ALU.is_equal, fill=0.0, base=0,
                                         channel_multiplier=-1),
       reads=[r_ones], writes=[r_ident])
    G.update(ident=ident, r_ident=r_ident, ones_f=ones_f, r_ones=r_ones)

    es_stage = ExitStack()
    stg_ring = Ring([es_stage.enter_context(nc.sbuf_tensor("cstg%d" % i, [128, 2, 2048], BF16))
                     for i in range(3)])
    bg = []
    bg_late = []
    pend = []

    def flush_store(keep=0):
        while len(pend) > keep:
            pend.pop(0)()

    def ag_in(src, dst, r, now=False, late=False, rows_sel=None, tag=""):
        rows, cols = src.shape
        src_ap = src.ap()
        if rows_sel is not None:
            src_ap = src_ap[rows_sel[0]:rows_sel[1], :]
            rows = rows_sel[1] - rows_sel[0]
        b = nc.dram_tensor(src.name + tag + "_b", [rows, cols], BF16)
        n = rows * cols // 2048
        if cols >= 2048:
            sv = src_ap.rearrange("r (a e) -> (r a) e", e=2048)
            bv = b.ap().rearrange("r (a e) -> (r a) e", e=2048)
        else:
            sv = src_ap.rearrange("(r a) c -> r (a c)", a=2048 // cols)
            bv = b.ap().rearrange("(r a) c -> r (a c)", a=2048 // cols)
        regs = []
        tasks = []
        for i in range(0, n, 256):
            m = min(256, n - i) // 128

            def piece(i=i, m=m):
                flush_store(2)
                st, r_st = stg_ring.next()
                rb = Reg()
                dma("pool", st[:, 0:m, :], sv[i:i + m * 128, :].rearrange("(t p) e -> p t e", p=128),
                    writes=[r_st])
                pend.append(lambda: dma("sp", bv[i:i + m * 128, :].rearrange("(t p) e -> p t e", p=128),
                                        st[:, 0:m, :], reads=[r_st], writes=[rb]))
                regs.append(rb)
            tasks.append(piece)
        def ag_task():
            flush_store()
            SC.allgather(b.ap(), dst.ap(), reads=regs, writes=[r])
        if late:
            bg_late.append(ag_task)
        else:
            tasks.append(ag_task)
        if now:
            for t in tasks:
                t()
        else:
            bg.extend(tasks)

    def bg_step(n=1):
        for _ in range(n):
            if bg:
                bg.pop(0)()
    G["bg_step"] = bg_step

    ag_in(xT_sh, xT_full, R["xT"], now=True)
    if debug != "p1":
        ag_in(wq_sh, G["wq_f"], R["wq"])
        ag_in(wk_sh, G["wk_f"], R["wk"])
        ag_in(wv_sh, G["wv_f"], R["wv"])
        ag_in(wo_sh, G["wo_f"], R["wo"])
        if debug != "p2":
            for hf, late in ((0, False), (1, True)):
                t = "ab"[hf]
                ag_in(w1_sh, G["w1_f" + t], R["w1" + t], late=late, rows_sel=(hf * 4 * D, (hf + 1) * 4 * D), tag=t)
                ag_in(w3_sh, G["w3_f" + t], R["w3" + t], late=late, rows_sel=(hf * 4 * D, (hf + 1) * 4 * D), tag=t)
                ag_in(w2_sh, G["w2_f" + t], R["w2" + t], late=late, rows_sel=(hf * 2048, (hf + 1) * 2048), tag=t)

    phase1(nc, SC, G)
    bg_step(len(bg))
    flush_store()
    SC.barrier()
    es_stage.close()
    SC.collective("ReduceScatter", ALU.add, G["part"].ap(), G["mixed"].ap(),
                  reads=G["r_part"], writes=[R["mixed"]])
    for t in bg_late:
        t()
    if debug == "p1":
        with nc.sbuf_tensor("dbg", [128, NTC, D], F32) as dbg:
            r = Reg()
            dma("sp", dbg[:], G["mixed"].ap().rearrange("(a p) d -> p a d", p=128), reads=[R["mixed"]],
                writes=[r])
            dma("sp", out.ap().rearrange("(a p) d -> p a d", p=128), dbg[:], reads=[r], writes=[R["out"]])
        SC.wait_all("sp", [R["out"]])
        return nc

    phase23(nc, SC, G, debug)
    SC.wait_all("sp", [R["out"]])
    print("instructions", SC.n_inst, "waits", SC.n_wait)
    return nc


def phase1(nc, SC, G):
    op, dma = SC.op, SC.dma
    xT_full, w_in_c, gm_par, gm_wT = G["xT_full"], G["w_in_c"], G["gm_par"], G["gm_wT"]
    ident, r_ident, ones_f, r_ones = G["ident"], G["r_ident"], G["ones_f"], G["r_ones"]
    r_xT = G["R"]["xT"]
    NCH = S // 512
    with ExitStack() as _es:
        w_sb = _es.enter_context(nc.sbuf_tensor("w_sb", [128, 16, 640], BF16))
        QT = _es.enter_context(nc.sbuf_tensor("QT", [128, S], BF16))
        KT = _es.enter_context(nc.sbuf_tensor("KT", [128, S], BF16))
        Vt = _es.enter_context(nc.sbuf_tensor("Vt", [128, S // 128, 128], BF16))
        mixA = _es.enter_context(nc.sbuf_tensor("mixA", [128, S], BF16))
        mixB = _es.enter_context(nc.sbuf_tensor("mixB", [128, S], BF16))
        gpar = _es.enter_context(nc.sbuf_tensor("gpar", [128, 384], F32))
        wsT = _es.enter_context(nc.sbuf_tensor("wsT", [128, 128], BF16))
        neghalf = _es.enter_context(nc.sbuf_tensor("neghalf", [128, 1], F32))
        r_w, r_gpar, r_wsT, r_nh = Reg(), Reg(), Reg(), Reg()
        r_Q = [Reg() for _ in range(NCH)]
        r_K = [Reg() for _ in range(NCH)]
        r_V = [Reg() for _ in range(NCH)]
        r_mA = [Reg() for _ in range(NCH)]
        r_mB = [Reg() for _ in range(S // 128)]
        dma("pool", w_sb[:], w_in_c.ap().rearrange("(dc p) n -> p dc n", p=128), writes=[r_w])
        dma("sp", gpar[:], gm_par.ap().partition_broadcast(128)[:, 0, :], writes=[r_gpar])
        dma("pool", wsT[:], gm_wT.ap(), writes=[r_wsT])
        op("dve", lambda e: e.memset(wsT[0:64, 64:128], 0.0), writes=[r_wsT])
        op("dve", lambda e: e.memset(neghalf[:], -0.5), writes=[r_nh])

        with ExitStack() as _es:
            xb0 = _es.enter_context(nc.sbuf_tensor("xb0", [128, 16, 512], BF16))
            xb1 = _es.enter_context(nc.sbuf_tensor("xb1", [128, 16, 512], BF16))
            xb2 = _es.enter_context(nc.sbuf_tensor("xb2", [128, 16, 512], BF16))
            g_x = _es.enter_context(nc.sbuf_tensor("g_x", [128, 2, 512], F32))
            g_t = _es.enter_context(nc.sbuf_tensor("g_t", [128, 2, 512], F32))
            g_s = _es.enter_context(nc.sbuf_tensor("g_s", [128, 2, 512], F32))
            UT = _es.enter_context(nc.sbuf_tensor("UT", [128, 2, 512], F32))
            v_x = _es.enter_context(nc.sbuf_tensor("v_x", [128, 2, 128], F32))
            v_t = _es.enter_context(nc.sbuf_tensor("v_t", [128, 2, 128], F32))
            v_s = _es.enter_context(nc.sbuf_tensor("v_s", [128, 2, 128], F32))
            v_g = _es.enter_context(nc.sbuf_tensor("v_g", [128, 2, 128], F32))
            v_n = _es.enter_context(nc.sbuf_tensor("v_n", [128, 2, 128], BF16))
            st6 = _es.enter_context(nc.sbuf_tensor("st6", [128, 2, 6], F32))
            st2 = _es.enter_context(nc.sbuf_tensor("st2", [128, 2, 2], F32))
            rstd = _es.enter_context(nc.sbuf_tensor("rstd", [128, 2, 1], F32))
            m_t = _es.enter_context(nc.sbuf_tensor("m_t", [128, 2, 128], F32))
            pp = _es.enter_context(nc.psum_tensor("pp", [128, 8, 512], F32))
            xring = Ring([xb0, xb1, xb2])
            pring = Ring([pp[:, i, :] for i in range(8)])
            rr = {n: [Reg(), Reg()] for n in
                  ("g_x", "g_t", "g_s", "UT", "v_x", "v_t", "v_s", "v_g", "v_n", "st6", "st2",
                   "rstd", "m_t")}
            xTv = xT_full.ap().rearrange("(dc p) t -> p dc t", p=128)
            gi = 0
            vi = 0

            def gelu(src_ps, r_ps, width, xs, ts, ss, dst, rxs, rts, rss, rdst):
                op("act", lambda e: e.activation(out=xs, in_=src_ps, func=AF.Copy),
                   reads=[r_ps], writes=[rxs])
                op("act", lambda e: e.activation(out=ts, in_=src_ps, func=AF.Square),
                   reads=[r_ps], writes=[rts])
                op("dve", lambda e: e.tensor_scalar(out=ts, in0=ts, scalar1=0.044715, scalar2=1.0,
                                                    op0=ALU.mult, op1=ALU.add),
                   reads=[rts], writes=[rts])
                op("dve", lambda e: e.tensor_tensor(out=ts, in0=ts, in1=xs, op=ALU.mult),
                   reads=[rts, rxs], writes=[rts])
                op("act", lambda e: e.activation(out=ss, in_=ts, func=AF.Sigmoid, scale=GELU_C),
                   reads=[rts], writes=[rss])
                op("dve", lambda e: e.tensor_tensor(out=dst, in0=ss, in1=xs, op=ALU.mult),
                   reads=[rss, rxs], writes=[rdst])

            for j in range(NCH):
                xb, r_xb = xring.next()
                tok = slice(j * 512, (j + 1) * 512)
                dma("sp", xb[:], xTv[:, :, tok], reads=[r_xT], writes=[r_xb])
                for which in range(3):
                    ps, r_ps = pring.next()
                    for dc in range(16):
                        op("pe", lambda e, dc=dc, ps=ps, which=which: e.matmul(
                            ps, lhsT=w_sb[:, dc, which * 128:(which + 1) * 128], rhs=xb[:, dc, :],
                            start=(dc == 0), stop=(dc == 15)),
                           reads=[r_w, r_xb], writes=[r_ps])
                    if which == 0:
                        op("act", lambda e, ps=ps: e.activation(out=QT[:, tok], in_=ps, func=AF.Copy,
                                                                 scale=128.0 ** -0.5),
                           reads=[r_ps], writes=[r_Q[j]])
                    elif which == 1:
                        op("dve", lambda e, ps=ps: e.tensor_copy(out=KT[:, tok], in_=ps),
                           reads=[r_ps], writes=[r_K[j]])
                    else:
                        g = gi % 2
                        gi += 1
                        gelu(ps, r_ps, 512, g_x[:, g, :], g_t[:, g, :], g_s[:, g, :], UT[:, g, :],
                             rr["g_x"][g], rr["g_t"][g], rr["g_s"][g], rr["UT"][g])
                        ug = g
                for b in range(4):
                    blk = j * 4 + b
                    if b % 2 == 0:
                        G["bg_step"]()
                    ps, r_ps = pring.next()
                    for dc in range(16):
                        op("pe", lambda e, dc=dc, ps=ps, b=b: e.matmul(
                            ps[:, 0:256], lhsT=xb[:, dc, b * 128:(b + 1) * 128], rhs=w_sb[:, dc, 384:640],
                            start=(dc == 0), stop=(dc == 15)),
                           reads=[r_w, r_xb], writes=[r_ps])
                    op("act", lambda e, ps=ps, blk=blk: e.activation(out=Vt[:, blk, :], in_=ps[:, 0:128],
                                                                     func=AF.Copy),
                       reads=[r_ps], writes=[r_V[j]])
                    v = vi % 2
                    vi += 1
                    gelu(ps[:, 128:256], r_ps, 128, v_x[:, v, :], v_t[:, v, :], v_s[:, v, :], v_g[:, v, :],
                         rr["v_x"][v], rr["v_t"][v], rr["v_s"][v], rr["v_g"][v])
                    op("dve", lambda e, v=v: e.bn_stats(out=st6[:, v, :], in_=v_g[:, v, :]),
                       reads=[rr["v_g"][v]], writes=[rr["st6"][v]])
                    op("dve", lambda e, v=v: e.bn_aggr(out=st2[:, v, :], in_=st6[:, v, :]),
                       reads=[rr["st6"][v]], writes=[rr["st2"][v]])
                    op("dve", lambda e, v=v: e.tensor_scalar(out=rstd[:, v, :], in0=st2[:, v, 1:2],
                                                             scalar1=LN_EPS, scalar2=None, op0=ALU.add),
                       reads=[rr["st2"][v]], writes=[rr["rstd"][v]])
                    op("pool", lambda e, v=v: e.tensor_tensor(out=rstd[:, v, :], in0=rstd[:, v, :],
                                                              in1=neghalf[:], op=ALU.pow),
                       reads=[rr["rstd"][v], r_nh], writes=[rr["rstd"][v]])
                    op("dve", lambda e, v=v: e.tensor_scalar(out=v_t[:, v, :], in0=v_g[:, v, :],
                                                             scalar1=st2[:, v, 0:1], scalar2=rstd[:, v, :],
                                                             op0=ALU.subtract, op1=ALU.mult),
                       reads=[rr["v_g"][v], rr["st2"][v], rr["rstd"][v]], writes=[rr["v_t"][v]])
                    op("pool", lambda e, v=v: e.tensor_tensor(out=v_t[:, v, :], in0=v_t[:, v, :],
                                                              in1=gpar[:, 0:128], op=ALU.mult),
                       reads=[rr["v_t"][v], r_gpar], writes=[rr["v_t"][v]])
                    op("pool", lambda e, v=v: e.tensor_tensor(out=v_n[:, v, :], in0=v_t[:, v, :],
                                                              in1=gpar[:, 128:256], op=ALU.add),
                       reads=[rr["v_t"][v], r_gpar], writes=[rr["v_n"][v]])
                    ps2, r_ps2 = pring.next()
                    op("pe", lambda e, v=v, ps2=ps2: e.matmul(ps2[:, 0:128], lhsT=v_n[:, v, :], rhs=wsT[:],
                                                              start=True, stop=True),
                       reads=[rr["v_n"][v], r_wsT], writes=[r_ps2])
                    op("dve", lambda e, v=v, ps2=ps2: e.tensor_tensor(out=m_t[:, v, :], in0=ps2[:, 0:128],
                                                                      in1=gpar[:, 256:384], op=ALU.add),
                       reads=[r_ps2, r_gpar], writes=[rr["m_t"][v]])
                    op("dve", lambda e, v=v, b=b, blk=blk, ug=ug: e.tensor_tensor(
                        out=mixA[:, blk * 128:(blk + 1) * 128], in0=m_t[:, v, :],
                        in1=UT[:, ug, b * 128:(b + 1) * 128], op=ALU.mult),
                       reads=[rr["m_t"][v], rr["UT"][ug]], writes=[r_mA[j]])

        SC.barrier()
        with ExitStack() as _es:
            e_sb = _es.enter_context(nc.sbuf_tensor("e_sb", [128, 4, 512], F32))
            L_sb = _es.enter_context(nc.sbuf_tensor("L_sb", [128, 4, 512], F32))
            C_sb = _es.enter_context(nc.sbuf_tensor("C_sb", [128, 4, 512], F32))
            c_sb = _es.enter_context(nc.sbuf_tensor("c_sb", [128, 4, 512], F32))
            A_sb = _es.enter_context(nc.sbuf_tensor("A_sb", [128, 4, 512], BF16))
            AT_sb = _es.enter_context(nc.sbuf_tensor("AT_sb", [128, 2, 512], BF16))
            zp = _es.enter_context(nc.psum_tensor("zp", [128, 3, 512], F32))
            atp = _es.enter_context(nc.psum_tensor("atp", [128, 2, 1024], BF16))
            otp = _es.enter_context(nc.psum_tensor("otp", [128, 2, 512], F32))
            NB = 4
            rz = [Reg() for _ in range(3)]
            re_ = [Reg() for _ in range(NB)]
            rL = [Reg() for _ in range(NB)]
            rC = [Reg() for _ in range(NB)]
            rc = [Reg() for _ in range(NB)]
            rA = [Reg() for _ in range(NB)]
            ratp = [Reg(), Reg()]
            rATs = [Reg(), Reg()]
            rotp = [Reg(), Reg()]
            units = []
            for i in range(S // 128):
                kb = i
                first = True
                while kb < S // 128:
                    n = min(4 - (kb % 4), S // 128 - kb)
                    units.append((i, kb, n, first, kb + n == S // 128))
                    first = False
                    kb += n

            def stage_z(u, k):
                i, kb, n, first, last = units[u]
                W = n * 128
                jq = i // 4
                kregs = [r_K[c] for c in sorted(set((kb + t) // 4 for t in range(n)))]
                op("pe", lambda e: e.matmul(zp[:, k, 0:W], lhsT=QT[:, i * 128:(i + 1) * 128],
                                            rhs=KT[:, kb * 128:kb * 128 + W], start=True, stop=True),
                   reads=[r_Q[jq]] + kregs, writes=[rz[k]])

            def stage_a1(u, k, kz):
                i, kb, n, first, last = units[u]
                W = n * 128
                op("act", lambda e: e.activation(out=e_sb[:, k, 0:W], in_=zp[:, kz, 0:W], func=AF.Exp),
                   reads=[rz[kz]], writes=[re_[k]])
                op("act", lambda e: e.activation(out=L_sb[:, k, 0:W], in_=e_sb[:, k, 0:W], func=AF.Ln,
                                                 bias=1.0),
                   reads=[re_[k]], writes=[rL[k]])
                if first:
                    for buf, r in ((L_sb, rL[k]), (e_sb, re_[k])):
                        op("pool", lambda e, buf=buf: e.affine_select(
                            out=buf[:, k, 0:128], in_=buf[:, k, 0:128], pattern=[[1, 128]],
                            compare_op=ALU.is_gt, fill=0.0, base=0, channel_multiplier=-1),
                           reads=[r], writes=[r])

            def stage_a2(u, k, kprev):
                i, kb, n, first, last = units[u]
                W = n * 128
                if first:
                    init = 0.0
                    extra = []
                else:
                    pi, pkb, pn, _, _ = units[u - 1]
                    init = C_sb[:, kprev, pn * 128 - 1:pn * 128]
                    extra = [rC[kprev]]
                op("dve", lambda e: e.tensor_tensor_scan(out=C_sb[:, k, 0:W], data0=ones_f[:, 0:W],
                                                         data1=L_sb[:, k, 0:W], initial=init,
                                                         op0=ALU.mult, op1=ALU.add),
                   reads=[rL[k], r_ones] + extra, writes=[rC[k]])
                op("act", lambda e: e.activation(out=c_sb[:, k, 0:W], in_=C_sb[:, k, 0:W], func=AF.Exp,
                                                 scale=-1.0),
                   reads=[rC[k]], writes=[rc[k]])
                op("pool", lambda e: e.tensor_tensor(out=A_sb[:, k, 0:W], in0=e_sb[:, k, 0:W],
                                                     in1=c_sb[:, k, 0:W], op=ALU.mult),
                   reads=[re_[k], rc[k]], writes=[rA[k]])

            def stage_b(u, k):
                i, kb, n, first, last = units[u]
                W = n * 128
                a = u % 2
                o = i % 2
                for t in range(n):
                    op("pe", lambda e, t=t: e.transpose(out=atp[:, a, t * 128:(t + 1) * 128],
                                                        in_=A_sb[:, k, t * 128:(t + 1) * 128],
                                                        identity=ident[:]),
                       reads=[rA[k], r_ident], writes=[ratp[a]])
                op("dve", lambda e: e.tensor_copy(out=AT_sb[:, a, 0:W], in_=atp[:, a, 0:W]),
                   reads=[ratp[a]], writes=[rATs[a]])
                for t in range(n):
                    op("pe", lambda e, t=t: e.matmul(otp[:, o, 0:128], lhsT=Vt[:, kb + t, :],
                                                     rhs=AT_sb[:, a, t * 128:(t + 1) * 128],
                                                     start=(first and t == 0), stop=(last and t == n - 1)),
                       reads=[r_V[(kb + t) // 4], rATs[a]], writes=[rotp[o]])
                if last:
                    op("act", lambda e: e.activation(out=mixB[:, i * 128:(i + 1) * 128], in_=otp[:, o, 0:128],
                                                     func=AF.Copy),
                       reads=[rotp[o]], writes=[r_mB[i]])

            NU = len(units)
            stage_z(0, 0)
            if NU > 1:
                stage_z(1, 1)
            stage_a1(0, 0, 0)
            for u in range(NU):
                if u % 8 == 0:
                    G["bg_step"]()
                if u + 2 < NU:
                    stage_z(u + 2, (u + 2) % 3)
                if u + 1 < NU:
                    stage_a1(u + 1, (u + 1) % NB, (u + 1) % 3)
                stage_a2(u, u % NB, (u - 1) % NB)
                if u >= 2:
                    stage_b(u - 2, (u - 2) % NB)
            stage_b(NU - 2, (NU - 2) % NB)
            stage_b(NU - 1, (NU - 1) % NB)

        SC.barrier()
        with ExitStack() as _es:
            wmo_sb = _es.enter_context(nc.sbuf_tensor("wmo_sb", [128, 2, D], BF16))
            stg = _es.enter_context(nc.sbuf_tensor("stg", [128, 3, D], F32))
            pq = _es.enter_context(nc.psum_tensor("pq", [128, 8, 512], F32))
            r_wmo = Reg()
            dma("pool", wmo_sb[:], G["wmo_c"].ap().rearrange("(a p) n -> p a n", p=128), writes=[r_wmo])
            pring = Ring([pq[:, i, :] for i in range(8)])
            sring = Ring([stg[:, i, :] for i in range(3)])
            partv = G["part"].ap()
            k = 0
            for blk in range(S // 128):
                st, r_st = sring.next()
                for cc in range(4):
                    ps, r_ps = pring.next()
                    op("pe", lambda e, ps=ps, cc=cc: e.matmul(
                        ps, lhsT=mixA[:, blk * 128:(blk + 1) * 128], rhs=wmo_sb[:, 0, cc * 512:(cc + 1) * 512],
                        start=True, stop=False), reads=[r_mA[blk // 4], r_wmo], writes=[r_ps])
                    op("pe", lambda e, ps=ps, cc=cc: e.matmul(
                        ps, lhsT=mixB[:, blk * 128:(blk + 1) * 128], rhs=wmo_sb[:, 1, cc * 512:(cc + 1) * 512],
                        start=False, stop=True), reads=[r_mB[blk], r_wmo], writes=[r_ps])
                    if k % 2 == 0:
                        op("act", lambda e, ps=ps, cc=cc, st=st: e.activation(
                            out=st[:, cc * 512:(cc + 1) * 512], in_=ps, func=AF.Copy),
                           reads=[r_ps], writes=[r_st])
                    else:
                        op("dve", lambda e, ps=ps, cc=cc, st=st: e.tensor_copy(
                            out=st[:, cc * 512:(cc + 1) * 512], in_=ps),
                           reads=[r_ps], writes=[r_st])
                    k += 1
                dma("sp", partv[blk * 128:(blk + 1) * 128, :], st, reads=[r_st], writes=[G["r_part"][blk]])
        SC.barrier()


def layer_norm(nc, SC, h, r_h, lnp, r_lnp, gi, tmp):
    op = SC.op
    st6, st2, rstd, neghalf = tmp["st6"], tmp["st2"], tmp["rstd"], tmp["neghalf"]
    for tc in range(NTC):
        k = tc % 2
        r6, r2, rr = tmp["r6"][k], tmp["r2"][k], tmp["rr"][k]
        for q in range(4):
            op("dve", lambda e, q=q: e.bn_stats(out=st6[:, k, q * 6:(q + 1) * 6], in_=h[:, tc, q * 512:(q + 1) * 512]),
               reads=[r_h[tc]], writes=[r6])
        op("dve", lambda e: e.bn_aggr(out=st2[:, k, :], in_=st6[:, k, :]), reads=[r6], writes=[r2])
        op("dve", lambda e: e.tensor_scalar(out=rstd[:, k, :], in0=st2[:, k, 1:2], scalar1=LN_EPS,
                                            scalar2=None, op0=ALU.add), reads=[r2], writes=[rr])
        op("pool", lambda e: e.tensor_tensor(out=rstd[:, k, :], in0=rstd[:, k, :], in1=neghalf[:],
                                             op=ALU.pow), reads=[rr, tmp["r_nh"]], writes=[rr])
        op("dve", lambda e: e.tensor_scalar(out=h[:, tc, :], in0=h[:, tc, :], scalar1=st2[:, k, 0:1],
                                            scalar2=rstd[:, k, :], op0=ALU.subtract, op1=ALU.mult),
           reads=[r_h[tc], r2, rr], writes=[r_h[tc]])
        op("pool", lambda e: e.tensor_tensor(out=h[:, tc, :], in0=h[:, tc, :], in1=lnp[:, 0, :], op=ALU.mult),
           reads=[r_h[tc], r_lnp], writes=[r_h[tc]])
        op("dve", lambda e: e.tensor_tensor(out=h[:, tc, :], in0=h[:, tc, :], in1=lnp[:, 1, :], op=ALU.add),
           reads=[r_h[tc], r_lnp], writes=[r_h[tc]])


def phase23(nc, SC, G, debug):
    op, dma = SC.op, SC.dma
    R = G["R"]
    ident, r_ident, ones_f, r_ones = G["ident"], G["r_ident"], G["ones_f"], G["r_ones"]
    lnv = G["ln_par"].ap().partition_broadcast(128)[:, 0, :]
    with ExitStack() as es0:
        h = es0.enter_context(nc.sbuf_tensor("h", [128, NTC, D], F32))
        wr = [es0.enter_context(nc.sbuf_tensor("wr%d" % i, [128, 8192], BF16)) for i in range(2)]
        st6 = es0.enter_context(nc.sbuf_tensor("ln_st6", [128, 2, 24], F32))
        st2 = es0.enter_context(nc.sbuf_tensor("ln_st2", [128, 2, 2], F32))
        rstd = es0.enter_context(nc.sbuf_tensor("ln_rstd", [128, 2, 1], F32))
        neghalf = es0.enter_context(nc.sbuf_tensor("ln_nh", [128, 1], F32))
        ps = es0.enter_context(nc.psum_tensor("ps", [128, 8, 512], F32))
        lt = dict(st6=st6, st2=st2, rstd=rstd, neghalf=neghalf, r6=[Reg(), Reg()], r2=[Reg(), Reg()],
                  rr=[Reg(), Reg()], r_nh=Reg())
        op("dve", lambda e: e.memset(neghalf[:], -0.5), writes=[lt["r_nh"]])
        r_h = [Reg() for _ in range(NTC)]
        wring = Ring(wr)
        bank = [(ps[:, i, :], Reg()) for i in range(8)]
        bi = [0]

        def nbank():
            b = bank[bi[0] % 8]
            bi[0] += 1
            return b

        def w16(buf):
            return buf[:].rearrange("p (a b) -> p a b", a=16)

        def load_w(dram, cc, r_src):
            buf, r_buf = wring.next()
            dma("sp", w16(buf), dram.ap().rearrange("(dc p) n -> p dc n", p=128)[:, :, cc * 512:(cc + 1) * 512],
                reads=[r_src], writes=[r_buf])
            return w16(buf), r_buf

        def do_ln(i):
            with ExitStack() as esl:
                lnp = esl.enter_context(nc.sbuf_tensor("lnp%d" % i, [128, 2, D], F32))
                r_lnp = Reg()
                dma("sp", lnp[:, 0, :], lnv[:, (2 * i) * D:(2 * i + 1) * D], writes=[r_lnp])
                dma("sp", lnp[:, 1, :], lnv[:, (2 * i + 1) * D:(2 * i + 2) * D], writes=[r_lnp])
                layer_norm(nc, SC, h, r_h, lnp, r_lnp, i, lt)
                SC.barrier()

        dma("sp", h[:], G["x_tok"].ap().rearrange("(tc p) d -> p tc d", p=128), writes=r_h)
        with ExitStack() as es:
            mxb = es.enter_context(nc.sbuf_tensor("mxb", [128, 2, D], F32))
            r_mxb = [Reg(), Reg()]
            mixv = G["mixed"].ap().rearrange("(tc p) d -> p tc d", p=128)
            for tc in range(NTC):
                k = tc % 2
                dma("sp", mxb[:, k, :], mixv[:, tc, :], reads=[R["mixed"]], writes=[r_mxb[k]])
                op("dve", lambda e, tc=tc, k=k: e.scalar_tensor_tensor(
                    out=h[:, tc, :], in0=h[:, tc, :], scalar=DN_ALPHA, in1=mxb[:, k, :],
                    op0=ALU.mult, op1=ALU.add), reads=[r_h[tc], r_mxb[k]], writes=[r_h[tc]])
            SC.barrier()
        do_ln(0)

        with ExitStack() as es:
            kT = es.enter_context(nc.sbuf_tensor("kT", [128, 16, MEM], BF16))
            v_sb = es.enter_context(nc.sbuf_tensor("v_sb", [128, 2, D], BF16))
            qT = es.enter_context(nc.sbuf_tensor("qT", [128, 16, TPC], BF16))
            r_kT, r_v, r_qT = Reg(), Reg(), Reg()
            with ExitStack() as es2:
                memT_sb = es2.enter_context(nc.sbuf_tensor("memT_sb", [128, 16, MEM], BF16))
                r_memT = Reg()
                dma("pool", memT_sb[:], G["memT"].ap().rearrange("(dc p) m -> p dc m", p=128), writes=[r_memT])
                for cc in range(4):
                    wb, r_wb = load_w(G["wk_f"], cc, R["wk"])
                    for g in range(4):
                        pb, r_pb = nbank()
                        for dc in range(16):
                            op("pe", lambda e, dc=dc, g=g, pb=pb, wb=wb: e.matmul(
                                pb[:, 0:MEM], lhsT=wb[:, dc, g * 128:(g + 1) * 128], rhs=memT_sb[:, dc, :],
                                start=(dc == 0), stop=(dc == 15)), reads=[r_wb, r_memT], writes=[r_pb])
                        op("act", lambda e, g=g, pb=pb, cc=cc: e.activation(
                            out=kT[:, cc * 4 + g, :], in_=pb[:, 0:MEM], func=AF.Copy), reads=[r_pb], writes=[r_kT])
                for cc in range(4):
                    wb, r_wb = load_w(G["wv_f"], cc, R["wv"])
                    for mc in range(2):
                        pb, r_pb = nbank()
                        for dc in range(16):
                            op("pe", lambda e, dc=dc, mc=mc, pb=pb, wb=wb: e.matmul(
                                pb, lhsT=memT_sb[:, dc, mc * 128:(mc + 1) * 128], rhs=wb[:, dc, :],
                                start=(dc == 0), stop=(dc == 15)), reads=[r_wb, r_memT], writes=[r_pb])
                        op("dve", lambda e, mc=mc, pb=pb, cc=cc: e.tensor_copy(
                            out=v_sb[:, mc, cc * 512:(cc + 1) * 512], in_=pb), reads=[r_pb], writes=[r_v])
                SC.barrier()
            with ExitStack() as es2:
                hb = es2.enter_context(nc.sbuf_tensor("hb", [128, 2, D], BF16))
                hT = es2.enter_context(nc.sbuf_tensor("hT", [128, 16, TPC], BF16))
                r_hb, r_hT = [Reg(), Reg()], Reg()
                for tc in range(NTC):
                    k = tc % 2
                    op("act", lambda e, tc=tc, k=k: e.activation(out=hb[:, k, :], in_=h[:, tc, :], func=AF.Copy),
                       reads=[r_h[tc]], writes=[r_hb[k]])
                    for half in range(2):
                        pb, r_pb = nbank()
                        pbb = pb.bitcast(BF16)
                        for j in range(8):
                            dc = half * 8 + j
                            op("pe", lambda e, j=j, dc=dc, k=k, pbb=pbb: e.transpose(
                                out=pbb[:, j * 128:(j + 1) * 128], in_=hb[:, k, dc * 128:(dc + 1) * 128],
                                identity=ident[:]), reads=[r_hb[k], r_ident], writes=[r_pb])
                        eng = "act" if half == 0 else "dve"
                        src = pbb.rearrange("p (a b) -> p a b", a=8)
                        dst = hT[:, half * 8:(half + 1) * 8, tc * 128:(tc + 1) * 128]
                        if eng == "act":
                            op("act", lambda e, src=src, dst=dst: e.activation(out=dst, in_=src, func=AF.Copy),
                               reads=[r_pb], writes=[r_hT])
                        else:
                            op("dve", lambda e, src=src, dst=dst: e.tensor_copy(out=dst, in_=src),
                               reads=[r_pb], writes=[r_hT])
                for cc in range(4):
                    wb, r_wb = load_w(G["wq_f"], cc, R["wq"])
                    for g in range(4):
                        for th in range(2):
                            pb, r_pb = nbank()
                            for dc in range(16):
                                op("pe", lambda e, dc=dc, g=g, th=th, pb=pb, wb=wb: e.matmul(
                                    pb, lhsT=wb[:, dc, g * 128:(g + 1) * 128],
                                    rhs=hT[:, dc, th * 512:(th + 1) * 512],
                                    start=(dc == 0), stop=(dc == 15)), reads=[r_wb, r_hT], writes=[r_pb])
                            op("act", lambda e, g=g, th=th, pb=pb, cc=cc: e.activation(
                                out=qT[:, cc * 4 + g, th * 512:(th + 1) * 512], in_=pb, func=AF.Copy,
                                scale=512.0 ** -0.5), reads=[r_pb], writes=[r_qT])
                SC.barrier()
            with ExitStack() as es2:
                oT = es2.enter_context(nc.sbuf_tensor("oT", [128, 16, TPC], BF16))
                p_f = es2.enter_context(nc.sbuf_tensor("p_f", [128, 2, 4 * MEM], F32))
                p_b = es2.enter_context(nc.sbuf_tensor("p_b", [128, 2, 4 * MEM], BF16))
                pT = es2.enter_context(nc.sbuf_tensor("pT", [128, 2, 8, 128], BF16))
                sm = es2.enter_context(nc.sbuf_tensor("sm", [128, 2, 16], F32))
                r_oT, r_pf, r_pb2, r_pT, r_sm = Reg(), [Reg(), Reg()], [Reg(), Reg()], [Reg(), Reg()], [Reg(), Reg()]
                for tc in range(NTC):
                    k = tc % 2
                    tok = slice(tc * 128, (tc + 1) * 128)
                    sc2 = []
                    for pair in range(2):
                        pb, r_pb = nbank()
                        sc2.append((pb, r_pb))
                        for hh in range(2):
                            hd = pair * 2 + hh
                            for j in range(4):
                                op("pe", lambda e, hd=hd, hh=hh, j=j, pb=pb: e.matmul(
                                    pb[:, hh * MEM:(hh + 1) * MEM], lhsT=qT[:, hd * 4 + j, tok],
                                    rhs=kT[:, hd * 4 + j, :], start=(j == 0), stop=(j == 3)),
                                   reads=[r_qT, r_kT], writes=[r_pb])
                    for pair in range(2):
                        pb, r_pb = sc2[pair]
                        op("dve", lambda e, pb=pb, pair=pair: e.tensor_reduce(
                            out=sm[:, k, pair * 2:pair * 2 + 2], in_=pb.rearrange("p (a b) -> p a b", a=2),
                            axis=AX.X, op=ALU.max), reads=[r_pb], writes=[r_sm[k]])
                    op("dve", lambda e: e.tensor_scalar(out=sm[:, k, 4:8], in0=sm[:, k, 0:4], scalar1=-1.0,
                                                        scalar2=None, op0=ALU.mult),
                       reads=[r_sm[k]], writes=[r_sm[k]])
                    for hd in range(4):
                        pb, r_pb = sc2[hd // 2]
                        hh = hd % 2
                        op("act", lambda e, hd=hd, hh=hh, pb=pb: e.activation(
                            out=p_f[:, k, hd * MEM:(hd + 1) * MEM], in_=pb[:, hh * MEM:(hh + 1) * MEM],
                            func=AF.Exp, bias=sm[:, k, 4 + hd:5 + hd], accum_out=sm[:, k, 8 + hd:9 + hd]),
                           reads=[r_pb, r_sm[k]], writes=[r_pf[k], r_sm[k]])
                    op("dve", lambda e: e.reciprocal(out=sm[:, k, 12:16], in_=sm[:, k, 8:12]),
                       reads=[r_sm[k]], writes=[r_sm[k]])
                    for hd in range(4):
                        op("dve", lambda e, hd=hd: e.tensor_scalar(
                            out=p_b[:, k, hd * MEM:(hd + 1) * MEM], in0=p_f[:, k, hd * MEM:(hd + 1) * MEM],
                            scalar1=sm[:, k, 12 + hd:13 + hd], scalar2=None, op0=ALU.mult),
                           reads=[r_pf[k], r_sm[k]], writes=[r_pb2[k]])
                    pb, r_pb = nbank()
                    pbb = pb.bitcast(BF16)
                    for j in range(8):
                        op("pe", lambda e, j=j, pbb=pbb: e.transpose(
                            out=pbb[:, j * 128:(j + 1) * 128], in_=p_b[:, k, j * 128:(j + 1) * 128],
                            identity=ident[:]), reads=[r_pb2[k], r_ident], writes=[r_pb])
                    op("act", lambda e, pbb=pbb: e.activation(
                        out=pT[:, k, :, :], in_=pbb.rearrange("p (a b) -> p a b", a=8), func=AF.Copy),
                       reads=[r_pb], writes=[r_pT[k]])
                    for q4 in range(4):
                        pb, r_pb = nbank()
                        for jj in range(4):
                            fc = q4 * 4 + jj
                            hd = fc // 4
                            for mc in range(2):
                                op("pe", lambda e, jj=jj, fc=fc, hd=hd, mc=mc, pb=pb: e.matmul(
                                    pb[:, jj * 128:(jj + 1) * 128], lhsT=v_sb[:, mc, fc * 128:(fc + 1) * 128],
                                    rhs=pT[:, k, hd * 2 + mc, :], start=(mc == 0), stop=(mc == 1)),
                                   reads=[r_v, r_pT[k]], writes=[r_pb])
                        src = pb.rearrange("p (a b) -> p a b", a=4)
                        dst = oT[:, q4 * 4:(q4 + 1) * 4, tok]
                        if q4 % 2 == 0:
                            op("dve", lambda e, src=src, dst=dst: e.tensor_copy(out=dst, in_=src),
                               reads=[r_pb], writes=[r_oT])
                        else:
                            op("act", lambda e, src=src, dst=dst: e.activation(out=dst, in_=src, func=AF.Copy),
                               reads=[r_pb], writes=[r_oT])
                for cc in range(4):
                    wb, r_wb = load_w(G["wo_f"], cc, R["wo"])
                    for tc in range(NTC):
                        pb, r_pb = nbank()
                        for fc in range(16):
                            op("pe", lambda e, fc=fc, tc=tc, pb=pb, wb=wb: e.matmul(
                                pb, lhsT=oT[:, fc, tc * 128:(tc + 1) * 128], rhs=wb[:, fc, :],
                                start=(fc == 0), stop=(fc == 15)), reads=[r_wb, r_oT], writes=[r_pb])
                        op("dve", lambda e, tc=tc, cc=cc, pb=pb: e.scalar_tensor_tensor(
                            out=h[:, tc, cc * 512:(cc + 1) * 512], in0=h[:, tc, cc * 512:(cc + 1) * 512],
                            scalar=DN_ALPHA, in1=pb, op0=ALU.mult, op1=ALU.add),
                           reads=[r_h[tc], r_pb], writes=[r_h[tc]])
                SC.barrier()
            do_ln(1)

        if debug == "p2":
            dma("sp", G["out"].ap().rearrange("(tc p) d -> p tc d", p=128), h[:], reads=r_h, writes=[R["out"]])
            return
        phase3(nc, SC, G, h, r_h, wr, nbank)
        do_ln(2)
        dma("sp", G["out"].ap().rearrange("(tc p) d -> p tc d", p=128), h[:], reads=r_h, writes=[R["out"]])
        SC.wait_all("sp", [R["out"]])


def phase3(nc, SC, G, h, r_h, wr2, nbank):
    op, dma = SC.op, SC.dma
    R = G["R"]
    ident, r_ident, ones_f, r_ones = G["ident"], G["r_ident"], G["ones_f"], G["r_ones"]
    I32 = mybir.dt.int32
    with ExitStack() as es:
        def sb(name, shape, dt):
            return es.enter_context(nc.sbuf_tensor(name, shape, dt))
        esA = ExitStack()

        def sa(name, shape, dt):
            return esA.enter_context(nc.sbuf_tensor(name, shape, dt))
        wr3 = sb("wr3", [128, 8192], BF16)
        wring = Ring(list(wr2) + [wr3])
        hb_all = sb("hb_all", [128, NTC, D], BF16)
        gsel = sb("gsel", [128, NTC, 8], F32)
        gsel_b = sb("gsel_b", [128, NTC, 8], BF16)
        gate = sb("gate", [128, NTC, 8], F32)
        gate_hi = sb("gate_hi", [128, NTC, 8], BF16)
        gate_lo = sb("gate_lo", [128, NTC, 8], BF16)
        slot = sb("slot", [128, NTC], F32)
        identf = sa("identf", [128, 128], F32)
        ustr = sa("ustr", [128, 128], BF16)
        ones_b = sa("ones_b", [128, 128], BF16)
        wrt = sa("wrt", [128, 16, 72], F32)
        brt = sa("brt", [128, 72], F32)
        hTf = sa("hTf", [128, 16, 128], F32)
        rt = sa("rt", [128, 160], F32)
        r_c = Reg()
        r_hb = [Reg() for _ in range(NTC)]
        r_hTf, r_rt, r_rout = Reg(), Reg(), Reg()
        r_P, r_PT, r_X, r_gs = Reg(), Reg(), Reg(), Reg()
        r_yacc, r_ybf = [Reg() for _ in range(NSB)], Reg()
        r_s, r_hid, r_hidT = [Reg(), Reg()], [Reg(), Reg()], [Reg(), Reg()]

        op("pool", lambda e: e.affine_select(out=identf[:], in_=ones_f[:, 0:128], pattern=[[1, 128]],
                                             compare_op=ALU.is_equal, fill=0.0, base=0, channel_multiplier=-1),
           reads=[r_ones], writes=[r_c])
        op("pool", lambda e: e.affine_select(out=ustr[:], in_=ones_f[:, 0:128], pattern=[[1, 128]],
                                             compare_op=ALU.is_gt, fill=0.0, base=0, channel_multiplier=-1),
           reads=[r_ones], writes=[r_c])
        op("dve", lambda e: e.tensor_copy(out=ones_b[:], in_=ones_f[:, 0:128]), reads=[r_ones], writes=[r_c])
        dma("sp", wrt[:], G["w_rt"].ap().rearrange("(dc p) n -> p dc n", p=128), writes=[r_c])
        dma("sp", brt[:], G["b_rt"].ap().partition_broadcast(128)[:, 0, :], writes=[r_c])

        for tc in range(NTC):
            op("act", lambda e, tc=tc: e.activation(out=hb_all[:, tc, :], in_=h[:, tc, :], func=AF.Copy),
               reads=[r_h[tc]], writes=[r_hb[tc]])
            for q4 in range(4):
                pb, r_pb = nbank()
                for j in range(4):
                    dc = q4 * 4 + j
                    op("pe", lambda e, j=j, dc=dc, pb=pb, tc=tc: e.transpose(
                        out=pb[:, j * 128:(j + 1) * 128], in_=h[:, tc, dc * 128:(dc + 1) * 128],
                        identity=identf[:]), reads=[r_h[tc], r_c], writes=[r_pb])
                src = pb.rearrange("p (a b) -> p a b", a=4)
                dst = hTf[:, q4 * 4:(q4 + 1) * 4, :]
                if q4 % 2 == 0:
                    op("dve", lambda e, src=src, dst=dst: e.tensor_copy(out=dst, in_=src), reads=[r_pb], writes=[r_hTf])
                else:
                    op("act", lambda e, src=src, dst=dst: e.activation(out=dst, in_=src, func=AF.Copy),
                       reads=[r_pb], writes=[r_hTf])
            pl, r_pl = nbank()
            for dc in range(16):
                op("pe", lambda e, dc=dc, pl=pl: e.matmul(pl[:, 0:72], lhsT=hTf[:, dc, :], rhs=wrt[:, dc, :],
                                                          start=(dc == 0), stop=(dc == 15)),
                   reads=[r_hTf, r_c], writes=[r_pl])
            op("pool", lambda e, tc=tc: e.tensor_scalar(out=h[:, tc, :], in0=h[:, tc, :], scalar1=DN_ALPHA,
                                                        scalar2=None, op0=ALU.mult),
               reads=[r_h[tc]], writes=[r_h[tc]])
            lg = rt[:, 0:72]
            V = lambda a, b: rt[:, a:b]
            gmax, ngmax, gsum, gval = V(72, 73), V(73, 74), V(74, 75), V(75, 76)
            gexp, el, oh1, el2, oh2 = V(76, 84), V(84, 92), V(92, 100), V(100, 108), V(108, 116)
            m1, m2, dd, ed, den, rden, ga, gb = (V(116 + i, 117 + i) for i in range(8))
            tmp8 = V(124, 132)
            D1 = lambda fn, extra_r=(), extra_w=(): op("dve", fn, reads=[r_rt] + list(extra_r),
                                                       writes=[r_rt] + list(extra_w))
            D1(lambda e: e.tensor_tensor(out=lg, in0=pl[:, 0:72], in1=brt[:], op=ALU.add), extra_r=[r_pl, r_c])
            D1(lambda e: e.reduce_max(out=gmax, in_=rt[:, 0:8], axis=AX.X))
            D1(lambda e, tc=tc: e.tensor_scalar(out=gsel[:, tc, :], in0=rt[:, 0:8], scalar1=gmax, scalar2=None,
                                                op0=ALU.is_equal), extra_w=[r_rout])
            D1(lambda e, tc=tc: e.tensor_copy(out=gsel_b[:, tc, :], in_=gsel[:, tc, :]), extra_r=[r_rout],
               extra_w=[r_rout])
            D1(lambda e: e.tensor_scalar(out=ngmax, in0=gmax, scalar1=-1.0, scalar2=None, op0=ALU.mult))
            op("act", lambda e: e.activation(out=gexp, in_=rt[:, 0:8], func=AF.Exp, bias=ngmax, accum_out=gsum),
               reads=[r_rt], writes=[r_rt])
            D1(lambda e: e.reciprocal(out=gval, in_=gsum))
            for g in range(8):
                if g == 0:
                    D1(lambda e, tc=tc: e.tensor_scalar(out=el, in0=rt[:, 8:16], scalar1=gsel[:, tc, 0:1],
                                                        scalar2=None, op0=ALU.mult), extra_r=[r_rout])
                else:
                    D1(lambda e, tc=tc, g=g: e.scalar_tensor_tensor(
                        out=el, in0=rt[:, 8 + g * 8:16 + g * 8], scalar=gsel[:, tc, g:g + 1], in1=el,
                        op0=ALU.mult, op1=ALU.add), extra_r=[r_rout])
            D1(lambda e: e.reduce_max(out=m1, in_=el, axis=AX.X))
            D1(lambda e: e.tensor_scalar(out=oh1, in0=el, scalar1=m1, scalar2=None, op0=ALU.is_equal))
            D1(lambda e: e.scalar_tensor_tensor(out=el2, in0=oh1, scalar=-1e30, in1=el, op0=ALU.mult, op1=ALU.add))
            D1(lambda e: e.reduce_max(out=m2, in_=el2, axis=AX.X))
            D1(lambda e: e.tensor_scalar(out=oh2, in0=el2, scalar1=m2, scalar2=None, op0=ALU.is_equal))
            D1(lambda e: e.tensor_tensor(out=dd, in0=m2, in1=m1, op=ALU.subtract))
            op("act", lambda e: e.activation(out=ed, in_=dd, func=AF.Exp), reads=[r_rt], writes=[r_rt])
            D1(lambda e: e.tensor_scalar(out=den, in0=ed, scalar1=1.0, scalar2=None, op0=ALU.add))
            D1(lambda e: e.reciprocal(out=rden, in_=den))
            D1(lambda e: e.tensor_tensor(out=ga, in0=rden, in1=gval, op=ALU.mult))
            D1(lambda e: e.tensor_tensor(out=gb, in0=ga, in1=ed, op=ALU.mult))
            D1(lambda e: e.tensor_scalar(out=tmp8, in0=oh1, scalar1=ga, scalar2=None, op0=ALU.mult))
            D1(lambda e, tc=tc: e.scalar_tensor_tensor(out=gate[:, tc, :], in0=oh2, scalar=gb, in1=tmp8,
                                                       op0=ALU.mult, op1=ALU.add), extra_w=[r_rout])
            D1(lambda e, tc=tc: e.tensor_copy(out=gate_hi[:, tc, :], in_=gate[:, tc, :]), extra_r=[r_rout],
               extra_w=[r_rout])
            D1(lambda e, tc=tc: e.tensor_tensor(out=gate_lo[:, tc, :], in0=gate[:, tc, :], in1=gate_hi[:, tc, :],
                                                op=ALU.subtract), extra_r=[r_rout], extra_w=[r_rout])
            pr, r_pr = nbank()
            op("pe", lambda e, pr=pr, tc=tc: e.matmul(pr[:, 0:8], lhsT=ustr[:], rhs=gsel_b[:, tc, :],
                                                      start=True, stop=(tc == 0)),
               reads=[r_c, r_rout], writes=[r_pr])
            for t2 in range(tc):
                op("pe", lambda e, pr=pr, t2=t2, tc=tc: e.matmul(pr[:, 0:8], lhsT=ones_b[:], rhs=gsel_b[:, t2, :],
                                                                 start=False, stop=(t2 == tc - 1)),
                   reads=[r_c, r_rout], writes=[r_pr])
            D1(lambda e, pr=pr, tc=tc: e.tensor_tensor(out=tmp8, in0=pr[:, 0:8], in1=gsel[:, tc, :], op=ALU.mult),
               extra_r=[r_pr, r_rout])
            D1(lambda e, tc=tc: e.reduce_sum(out=slot[:, tc:tc + 1], in_=tmp8, axis=AX.X), extra_w=[r_rout])

        SC.barrier()
        esA.close()
        iota_f = sb("iota_f", [128, CAP], F32)
        P = sb("P", [128, NTC, CAP], BF16)
        PT = sb("PT", [128, NSB, TPC], BF16)
        XgT = sb("XgT", [128, 16, CAP], BF16)
        gs = sb("gs", [128, NSB, 8], F32)
        y_acc = sb("y_acc", [128, NSB, D], F32)
        y_bf = sb("y_bf", [128, NSB, D], BF16)
        s_sb = sb("s_sb", [128, 2, 512], F32)
        hid = sb("hid", [128, 2, 512], BF16)
        hidT = sb("hidT", [128, 2, 4, 128], BF16)
        r_c2 = Reg()
        op("dve", lambda e: e.tensor_tensor_scan(out=iota_f[:], data0=ones_f[:, 0:CAP], data1=ones_f[:, 0:CAP],
                                                 initial=-1.0, op0=ALU.mult, op1=ALU.add),
           reads=[r_ones], writes=[r_c2])

        def w16(buf):
            return buf[:].rearrange("p (a b) -> p a b", a=16)

        def w4(buf):
            return buf[:].rearrange("p (a b) -> p a b", a=4)

        kk = 0
        for half in range(2):
          hs_ = "ab"[half]
          w1v, w3v, w2v = G["w1_f" + hs_].ap(), G["w3_f" + hs_].ap(), G["w2_f" + hs_].ap()
          rw1, rw3, rw2 = R["w1" + hs_], R["w3" + hs_], R["w2" + hs_]
          for g in range(8):
              for tc in range(NTC):
                  op("dve", lambda e, tc=tc, g=g: e.tensor_scalar(
                      out=P[:, tc, :], in0=iota_f[:], scalar1=slot[:, tc:tc + 1], scalar2=gsel[:, tc, g:g + 1],
                      op0=ALU.is_equal, op1=ALU.mult), reads=[r_c2, r_rout], writes=[r_P])
              for dc in range(16):
                  pb, r_pb = nbank()
                  for tc in range(NTC):
                      op("pe", lambda e, dc=dc, tc=tc, pb=pb: e.matmul(
                          pb[:, 0:CAP], lhsT=hb_all[:, tc, dc * 128:(dc + 1) * 128], rhs=P[:, tc, :],
                          start=(tc == 0), stop=(tc == NTC - 1)), reads=[r_hb[tc], r_P], writes=[r_pb])
                  if dc % 2 == 0:
                      op("act", lambda e, dc=dc, pb=pb: e.activation(out=XgT[:, dc, :], in_=pb[:, 0:CAP], func=AF.Copy),
                         reads=[r_pb], writes=[r_X])
                  else:
                      op("dve", lambda e, dc=dc, pb=pb: e.tensor_copy(out=XgT[:, dc, :], in_=pb[:, 0:CAP]),
                         reads=[r_pb], writes=[r_X])
              for sbk in range(NSB):
                  pb, r_pb = nbank()
                  n = 0
                  for gg in (gate_hi, gate_lo):
                      for tc in range(NTC):
                          op("pe", lambda e, tc=tc, sbk=sbk, gg=gg, pb=pb, n=n: e.matmul(
                              pb[:, 0:8], lhsT=P[:, tc, sbk * 128:(sbk + 1) * 128], rhs=gg[:, tc, :],
                              start=(n == 0), stop=(n == 2 * NTC - 1)), reads=[r_P, r_rout], writes=[r_pb])
                          n += 1
                  op("dve", lambda e, sbk=sbk, pb=pb: e.tensor_copy(out=gs[:, sbk, :], in_=pb[:, 0:8]),
                     reads=[r_pb], writes=[r_gs])
                  pb, r_pb = nbank()
                  pbb = pb.bitcast(BF16)
                  for tc in range(NTC):
                      op("pe", lambda e, tc=tc, sbk=sbk, pbb=pbb: e.transpose(
                          out=pbb[:, tc * 128:(tc + 1) * 128], in_=P[:, tc, sbk * 128:(sbk + 1) * 128],
                          identity=ident[:]), reads=[r_P, r_ident], writes=[r_pb])
                  op("act", lambda e, sbk=sbk, pbb=pbb: e.activation(out=PT[:, sbk, :], in_=pbb, func=AF.Copy),
                     reads=[r_pb], writes=[r_PT])
              for e4 in range(4):
                  ee = half * 4 + e4
                  E = g * 4 + e4
                  b1, r_b1 = wring.next()
                  dma("sp", w16(b1), w1v[E * D:(E + 1) * D, :].rearrange("(dc p) n -> p dc n", p=128),
                      reads=[rw1], writes=[r_b1])
                  b3, r_b3 = wring.next()
                  dma("sp", w16(b3), w3v[E * D:(E + 1) * D, :].rearrange("(dc p) n -> p dc n", p=128),
                      reads=[rw3], writes=[r_b3])
                  b2, r_b2 = wring.next()
                  dma("sp", w4(b2), w2v[E * 512:(E + 1) * 512, :].rearrange("(kc p) n -> p kc n", p=128),
                      reads=[rw2], writes=[r_b2])
                  W1, W3, W2 = w16(b1), w16(b3), w4(b2)
                  for sbk in range(NSB):
                      k = kk % 2
                      kk += 1
                      sl = slice(sbk * 128, (sbk + 1) * 128)
                      p1, r_p1 = nbank()
                      for dc in range(16):
                          op("pe", lambda e, dc=dc, p1=p1: e.matmul(p1, lhsT=XgT[:, dc, sl], rhs=W1[:, dc, :],
                                                                    start=(dc == 0), stop=(dc == 15)),
                             reads=[r_X, r_b1], writes=[r_p1])
                      p3, r_p3 = nbank()
                      for dc in range(16):
                          op("pe", lambda e, dc=dc, p3=p3: e.matmul(p3, lhsT=XgT[:, dc, sl], rhs=W3[:, dc, :],
                                                                    start=(dc == 0), stop=(dc == 15)),
                             reads=[r_X, r_b3], writes=[r_p3])
                      op("act", lambda e, p1=p1: e.activation(out=s_sb[:, k, :], in_=p1, func=AF.Silu),
                         reads=[r_p1], writes=[r_s[k]])
                      op("dve", lambda e, p3=p3: e.scalar_tensor_tensor(
                          out=hid[:, k, :], in0=p3, scalar=gs[:, sbk, ee:ee + 1], in1=s_sb[:, k, :],
                          op0=ALU.mult, op1=ALU.mult), reads=[r_p3, r_gs, r_s[k]], writes=[r_hid[k]])
                      pt, r_pt = nbank()
                      ptb = pt.bitcast(BF16)
                      for kc in range(4):
                          op("pe", lambda e, kc=kc, ptb=ptb: e.transpose(
                              out=ptb[:, kc * 128:(kc + 1) * 128], in_=hid[:, k, kc * 128:(kc + 1) * 128],
                              identity=ident[:]), reads=[r_hid[k], r_ident], writes=[r_pt])
                      op("act", lambda e, ptb=ptb: e.activation(
                          out=hidT[:, k, :, :], in_=ptb[:, 0:512].rearrange("p (a b) -> p a b", a=4), func=AF.Copy),
                         reads=[r_pt], writes=[r_hidT[k]])
                      for cc in range(4):
                          py, r_py = nbank()
                          for kc in range(4):
                              op("pe", lambda e, kc=kc, cc=cc, py=py: e.matmul(
                                  py, lhsT=hidT[:, k, kc, :], rhs=W2[:, kc, cc * 512:(cc + 1) * 512],
                                  start=(kc == 0), stop=(kc == 3)), reads=[r_hidT[k], r_b2], writes=[r_py])
                          ya = y_acc[:, sbk, cc * 512:(cc + 1) * 512]
                          if e4 == 0:
                              op("dve", lambda e, ya=ya, py=py: e.tensor_copy(out=ya, in_=py),
                                 reads=[r_py], writes=[r_yacc[sbk]])
                          else:
                              op("dve", lambda e, ya=ya, py=py: e.tensor_tensor(out=ya, in0=ya, in1=py, op=ALU.add),
                                 reads=[r_py, r_yacc[sbk]], writes=[r_yacc[sbk]])
              for sbk in range(NSB):
                  op("act", lambda e, sbk=sbk: e.activation(out=y_bf[:, sbk, :], in_=y_acc[:, sbk, :], func=AF.Copy),
                     reads=[r_yacc[sbk]], writes=[r_ybf])
              for tc in range(NTC):
                  for cc in range(4):
                      pb, r_pb = nbank()
                      for sbk in range(NSB):
                          op("pe", lambda e, sbk=sbk, tc=tc, cc=cc, pb=pb: e.matmul(
                              pb, lhsT=PT[:, sbk, tc * 128:(tc + 1) * 128], rhs=y_bf[:, sbk, cc * 512:(cc + 1) * 512],
                              start=(sbk == 0), stop=(sbk == NSB - 1)), reads=[r_PT, r_ybf], writes=[r_pb])
                      hs = h[:, tc, cc * 512:(cc + 1) * 512]
                      op("dve", lambda e, hs=hs, pb=pb: e.tensor_tensor(out=hs, in0=hs, in1=pb, op=ALU.add),
                         reads=[r_pb, r_h[tc]], writes=[r_h[tc]])
        SC.barrier()


def make_in_maps(inp):
    f = lambda a: np.ascontiguousarray(np.asarray(a, dtype=np.float32))
    x = f(inp["x"])[0]
    xr = x[::-1]
    xT = f(xr.T)
    w_in = f(inp["w_in"])[0]
    gm_ln_g, gm_ln_b = f(inp["gm_ln_g"])[0], f(inp["gm_ln_b"])[0]
    gm_w_s, gm_b_s = f(inp["gm_w_s"])[0], f(inp["gm_b_s"])[0]
    wmo = f(inp["w_mix_out"])[0]
    wq, wk, wv, wo = (f(inp[k])[0] for k in ("mem_w_q", "mem_w_k", "mem_w_v", "mem_w_o"))
    memT = f(f(inp["mem"])[0].T)
    ln_par = f(np.concatenate([f(inp[k])[0] for k in
                               ("ln1_g", "ln1_b", "ln2_g", "ln2_b", "ln3_g", "ln3_b")])[None, :])
    w_rt = f(np.concatenate([f(inp["w_group"])[0],
                             f(inp["w_router"])[0].transpose(1, 0, 2).reshape(D, 64)], axis=1))
    b_rt = f(np.concatenate([f(inp["b_group"])[0], f(inp["b_router"])[0].reshape(64)])[None, :])
    w1, w3, w2 = f(inp["w1"])[0], f(inp["w3"])[0], f(inp["w2"])[0]
    maps = []
    R = D // NCORES
    for c in range(NCORES):
        cols = np.concatenate([2048 + c * 128 + np.arange(128),
                               3072 + c * 128 + np.arange(128),
                               c * 128 + np.arange(128),
                               4096 + c * 128 + np.arange(128),
                               1024 + c * 128 + np.arange(128)])
        m = {
            "xT_sh": f(xT[c * R:(c + 1) * R]),
            "w_in_c": f(w_in[:, cols]),
            "gm_par": f(np.concatenate([gm_ln_g[c], gm_ln_b[c], gm_b_s[c][::-1]])[None, :]),
            "gm_wT": f(gm_w_s[c][::-1, ::-1].T),
            "x_tok": f(xr[c * TPC:(c + 1) * TPC]),
            "wmo_c": f(np.concatenate([wmo[c * 128:(c + 1) * 128],
                                       wmo[1024 + c * 128:1024 + (c + 1) * 128]], axis=0)),
            "wq_sh": f(wq[c * R:(c + 1) * R]),
            "wk_sh": f(wk[c * R:(c + 1) * R]),
            "wv_sh": f(wv[c * R:(c + 1) * R]),
            "wo_sh": f(wo[c * R:(c + 1) * R]),
            "memT": memT, "ln_par": ln_par, "w_rt": w_rt, "b_rt": b_rt,
            "w1_sh": f(w1[c * 8:(c + 1) * 8].reshape(8 * D, 512)),
            "w3_sh": f(w3[c * 8:(c + 1) * 8].reshape(8 * D, 512)),
            "w2_sh": f(w2[c * 8:(c + 1) * 8].reshape(8 * 512, D)),
        }
        maps.append(m)
    return maps


def kernel(**inputs):
    nc = build()
    maps = make_in_maps(inputs)
    res = run_bass_kernel_spmd(nc, maps, core_ids=list(range(NCORES)))
    outs = [np.asarray(res.results[c]["out"], dtype=np.float32) for c in range(NCORES)]
    full = np.concatenate(outs, axis=0)[::-1]
    return np.ascontiguousarray(full)[None, :, :]
```
